# Optimizing a Trainium2 kernel written in Bass

```python
import math
import jax, jax.numpy as jnp
from jax import lax
import numpy as np

D_MODEL = 2048
BATCH = 1
SEQ = 16384
DEPTH = 2

N_A_LAYERS = DEPTH // 2
N_B_LAYERS = DEPTH - N_A_LAYERS
BLK = 128
ROPE_THETA = 500000.0
NORM_EPS = 1e-6

A_HEAD_DIM = 128
A_HEADS = D_MODEL // A_HEAD_DIM
A_ROT_DIM = A_HEAD_DIM // 4
A_GROUPS = ((128, 1), (512, 4), (2048, 16))
N_A_GROUPS = len(A_GROUPS)
A_GROUP_WIDTH = A_HEADS * A_HEAD_DIM

B_HEADS = D_MODEL // 128
B_NOPE_DIM = 128
B_ROPE_DIM = 64
B_V_DIM = 128
B_Q_LORA = 512
B_KV_LORA = 512

FFN_HIDDEN = -(-(8 * D_MODEL) // (3 * 256)) * 256

kernel_name = 'yoco_dilated_mla_hybrid'


def rms_norm(x, g):
    x32 = x.astype(jnp.float32)
    y = x32 * lax.rsqrt(jnp.mean(x32 * x32, axis=-1, keepdims=True) + NORM_EPS)
    return (y * g.astype(jnp.float32)).astype(x.dtype)


def rope_angles(seq_len, dim):
    inv_freq = ROPE_THETA ** (-jnp.arange(0, dim, 2, dtype=jnp.float32) / dim)
    ang = jnp.arange(seq_len, dtype=jnp.float32)[:, None] * inv_freq[None, :]
    return jnp.cos(ang), jnp.sin(ang)


def apply_rope(x, cos, sin):
    half = x.shape[-1] // 2
    x32 = x.astype(jnp.float32)
    x1, x2 = x32[..., :half], x32[..., half:]
    c = cos[None, :, None, :]
    s = sin[None, :, None, :]
    return jnp.concatenate([x1 * c - x2 * s, x2 * c + x1 * s], axis=-1).astype(x.dtype)


def partial_rope(x, cos, sin):
    return jnp.concatenate([apply_rope(x[..., :A_ROT_DIM], cos, sin), x[..., A_ROT_DIM:]], axis=-1)


def dilated_window_branch(q, k, v, window, dilation):
    b, s, h, e = q.shape
    span = window // dilation
    n_prev = -(-span // BLK)
    chunk = dilation * BLK
    sp = -(-s // chunk) * chunk
    m = sp // dilation
    nb = m // BLK

    def to_blocks(t):
        t = jnp.pad(t, ((0, 0), (0, sp - s), (0, 0), (0, 0)))
        t = t.reshape(b, m, dilation, h, e).transpose(0, 2, 1, 3, 4)
        return t.reshape(b, dilation, nb, BLK, h, e)

    def band(t):
        tp = jnp.pad(t, ((0, 0), (0, 0), (n_prev, 0), (0, 0), (0, 0), (0, 0)))
        return jnp.concatenate([tp[:, :, j:j + nb] for j in range(n_prev + 1)], axis=3)

    qb = to_blocks(q)
    kband = band(to_blocks(k))
    vband = band(to_blocks(v))
    kb_len = (n_prev + 1) * BLK
    qi = jnp.arange(BLK)[:, None]
    kj = jnp.arange(kb_len)[None, :]
    dist = qi + n_prev * BLK - kj
    key_sub = (jnp.arange(nb)[:, None, None] - n_prev) * BLK + kj[None]
    valid = (dist >= 0)[None] & (dist <= span)[None] & (key_sub >= 0)

    scale = e ** -0.5
    sc = jnp.einsum('bdnqhe,bdnkhe->bdnhqk', qb, kband, preferred_element_type=jnp.float32) * scale
    sc = jnp.where(valid[None, None, :, None], sc, -jnp.inf)
    lse = jax.nn.logsumexp(sc, axis=-1)
    p = jnp.exp(sc - lse[..., None])
    out = jnp.einsum('bdnhqk,bdnkhe->bdnqhe', p.astype(v.dtype), vband, preferred_element_type=jnp.float32)
    out = out.reshape(b, dilation, m, h, e).transpose(0, 2, 1, 3, 4).reshape(b, sp, h, e)[:, :s]
    lse = lse.transpose(0, 1, 2, 4, 3).reshape(b, dilation, m, h).transpose(0, 2, 1, 3).reshape(b, sp, h)[:, :s]
    return out, lse


def dilated_attention(xn, w_qkv, w_o, cos, sin):
    b, s, _ = xn.shape
    qkv = (xn @ w_qkv).reshape(b, s, N_A_GROUPS, 3, A_HEADS, A_HEAD_DIM)
    outs, lses = [], []
    for g, (window, dilation) in enumerate(A_GROUPS):
        q = partial_rope(qkv[:, :, g, 0], cos, sin)
        k = partial_rope(qkv[:, :, g, 1], cos, sin)
        o, l = dilated_window_branch(q, k, qkv[:, :, g, 2], window, dilation)
        outs.append(o)
        lses.append(l)
    wts = jax.nn.softmax(jnp.stack(lses), axis=0)
    o = jnp.sum(wts[..., None] * jnp.stack(outs), axis=0)
    return o.astype(xn.dtype).reshape(b, s, A_GROUP_WIDTH) @ w_o


def mla_shared_kv(h, kv_norm_g, w_kv_a, kv_a_norm_g, w_kv_b, cos, sin):
    b, s, _ = h.shape
    ckv = rms_norm(h, kv_norm_g) @ w_kv_a
    c = rms_norm(ckv[..., :B_KV_LORA], kv_a_norm_g)
    k_pe = apply_rope(ckv[..., None, B_KV_LORA:], cos, sin)[:, :, 0]
    kv = (c @ w_kv_b).reshape(b, s, B_HEADS, B_NOPE_DIM + B_V_DIM)
    return kv[..., :B_NOPE_DIM], k_pe, kv[..., B_NOPE_DIM:]


def mla_attention(xn, k_nope, k_pe, v, w_q_a, q_a_norm_g, w_q_b, w_o, cos, sin):
    b, s, _ = xn.shape
    cq = rms_norm(xn @ w_q_a, q_a_norm_g)
    q = (cq @ w_q_b).reshape(b, s, B_HEADS, B_NOPE_DIM + B_ROPE_DIM)
    q_nope = q[..., :B_NOPE_DIM]
    q_pe = apply_rope(q[..., B_NOPE_DIM:], cos, sin)
    nb = s // BLK
    scale = (B_NOPE_DIM + B_ROPE_DIM) ** -0.5
    key_pos = jnp.arange(s)

    def to_blocks(t):
        return t.reshape(b, nb, BLK, *t.shape[2:]).swapaxes(0, 1)

    def block_attn(args):
        qn, qp, i = args
        sc = (jnp.einsum('bqhe,bkhe->bhqk', qn, k_nope, preferred_element_type=jnp.float32)
              + jnp.einsum('bqhr,bkr->bhqk', qp, k_pe, preferred_element_type=jnp.float32)) * scale
        q_pos = i * BLK + jnp.arange(BLK)
        sc = jnp.where(key_pos[None, :] <= q_pos[:, None], sc, -jnp.inf)
        p = jax.nn.softmax(sc, axis=-1)
        return jnp.einsum('bhqk,bkhe->bqhe', p.astype(v.dtype), v, preferred_element_type=jnp.float32).astype(v.dtype)

    o = lax.map(block_attn, (to_blocks(q_nope), to_blocks(q_pe), jnp.arange(nb)))
    o = o.swapaxes(0, 1).reshape(b, s, B_HEADS * B_V_DIM)
    return o @ w_o


def swiglu(xn, w_gu, w_down):
    g, u = jnp.split(xn @ w_gu, 2, axis=-1)
    return (jax.nn.silu(g) * u) @ w_down


def setup_inputs(seed: int = 0) -> dict:
    key = jax.random.key(seed)
    ks = jax.random.split(key, 16)

    def w(k, shape, fan_in):
        return jax.random.normal(k, shape, jnp.float32) * fan_in ** -0.5

    def gain(k, shape):
        return 1.0 + 0.02 * jax.random.normal(k, shape, jnp.float32)

    a_cols = N_A_GROUPS * 3 * A_GROUP_WIDTH
    return {
        'x': jax.random.normal(ks[0], (BATCH, SEQ, D_MODEL), jnp.float32),
        'attn_norm_g': gain(ks[1], (DEPTH, D_MODEL)),
        'ffn_norm_g': gain(ks[2], (DEPTH, D_MODEL)),
        'a_w_qkv': w(ks[3], (N_A_LAYERS, D_MODEL, a_cols), D_MODEL),
        'a_w_o': w(ks[4], (N_A_LAYERS, A_GROUP_WIDTH, D_MODEL), A_GROUP_WIDTH),
        'kv_norm_g': gain(ks[5], (D_MODEL,)),
        'b_w_kv_a': w(ks[6], (D_MODEL, B_KV_LORA + B_ROPE_DIM), D_MODEL),
        'b_kv_a_norm_g': gain(ks[7], (B_KV_LORA,)),
        'b_w_kv_b': w(ks[8], (B_KV_LORA, B_HEADS * (B_NOPE_DIM + B_V_DIM)), B_KV_LORA),
        'b_w_q_a': w(ks[9], (N_B_LAYERS, D_MODEL, B_Q_LORA), D_MODEL),
        'b_q_a_norm_g': gain(ks[10], (N_B_LAYERS, B_Q_LORA)),
        'b_w_q_b': w(ks[11], (N_B_LAYERS, B_Q_LORA, B_HEADS * (B_NOPE_DIM + B_ROPE_DIM)), B_Q_LORA),
        'b_w_o': w(ks[12], (N_B_LAYERS, B_HEADS * B_V_DIM, D_MODEL), B_HEADS * B_V_DIM),
        'ffn_w_gu': w(ks[13], (DEPTH, D_MODEL, 2 * FFN_HIDDEN), D_MODEL),
        'ffn_w_down': w(ks[14], (DEPTH, FFN_HIDDEN, D_MODEL), FFN_HIDDEN),
        'final_norm_g': gain(ks[15], (D_MODEL,)),
    }


def reference(x, attn_norm_g, ffn_norm_g, a_w_qkv, a_w_o, kv_norm_g, b_w_kv_a, b_kv_a_norm_g, b_w_kv_b,
              b_w_q_a, b_q_a_norm_g, b_w_q_b, b_w_o, ffn_w_gu, ffn_w_down, final_norm_g):
    s = x.shape[1]
    cos_a, sin_a = rope_angles(s, A_ROT_DIM)
    cos_b, sin_b = rope_angles(s, B_ROPE_DIM)
    h = x
    k_nope = k_pe = v_shared = None
    for layer in range(DEPTH):
        if layer == N_A_LAYERS:
            k_nope, k_pe, v_shared = mla_shared_kv(h, kv_norm_g, b_w_kv_a, b_kv_a_norm_g, b_w_kv_b, cos_b, sin_b)
        xn = rms_norm(h, attn_norm_g[layer])
        if layer < N_A_LAYERS:
            h = h + dilated_attention(xn, a_w_qkv[layer], a_w_o[layer], cos_a, sin_a)
        else:
            j = layer - N_A_LAYERS
            h = h + mla_attention(xn, k_nope, k_pe, v_shared, b_w_q_a[j], b_q_a_norm_g[j], b_w_q_b[j], b_w_o[j], cos_b, sin_b)
        h = h + swiglu(rms_norm(h, ffn_norm_g[layer]), ffn_w_gu[layer], ffn_w_down[layer])
    return rms_norm(h, final_norm_g)
```

```python
import contextlib
import numpy as np
import ml_dtypes
import concourse.bass as bass
import concourse.mybir as mybir
from concourse.bass_utils import run_bass_kernel_spmd

F32 = mybir.dt.float32
BF16 = mybir.dt.bfloat16
AF = mybir.ActivationFunctionType
ALU = mybir.AluOpType

NCORES = 8
S = 16384
D = 2048
TOK = S // NCORES
TT = 512
FF = 5632
NF = FF // 128
EPS = 1e-6
ROPE_THETA = 500000.0

SEM_CHUNK = 16000
DMA_SEM_MAX = 30000


class _Op:
    __slots__ = ("eng", "idx", "fn", "deps", "dma", "needs_inc", "ms", "sem", "target", "prev_target", "cc")

    def __init__(self, eng, idx, fn, deps, dma):
        self.eng = eng
        self.idx = idx
        self.fn = fn
        self.deps = deps
        self.dma = dma
        self.needs_inc = False
        self.ms = None
        self.sem = None
        self.target = None
        self.prev_target = None
        self.cc = False


class Prog:
    ENGS = ("pe", "act", "dve", "pool", "sp")

    def __init__(self, nc):
        self.nc = nc
        self.ops = {e: [] for e in self.ENGS}
        self.last_writer = {}
        self.readers = {}

    def add(self, eng, fn, reads=(), writes=(), dma=False):
        pr = [b for b in reads if isinstance(b, tuple) and b[0] == "pb"]
        if pr:
            reads = [b for b in reads if not (isinstance(b, tuple) and b[0] == "pb")]
            writes = list(writes) + pr
        deps = set()
        for b in reads:
            w = self.last_writer.get(b)
            if w is not None:
                deps.add(w)
        for b in writes:
            w = self.last_writer.get(b)
            if w is not None:
                deps.add(w)
            rs = self.readers.get(b)
            if rs:
                deps.update(rs.values())
        idx = len(self.ops[eng])
        key = (eng, idx)
        if eng == "pe" and not dma:
            pe_ops = self.ops["pe"]
            deps = {d for d in deps if not (d[0] == "pe" and not pe_ops[d[1]].dma)}
        op = _Op(eng, idx, fn, deps, dma)
        self.ops[eng].append(op)
        for b in writes:
            self.last_writer[b] = key
            self.readers[b] = {}
        for b in reads:
            rd = self.readers.setdefault(b, {})
            if dma:
                rd[key] = key
            else:
                rd[eng] = key
        return op

    def dma(self, out, in_, reads=(), writes=(), q="sp"):
        return self.add(q, lambda e: e.dma_start(out=out, in_=in_), reads, writes, dma=True)

    def collective(self, fn, reads=(), writes=()):
        op = self.add("pool", fn, reads, writes, dma=True)
        op.cc = True
        return op

    def barrier(self):
        deps = set()
        b0 = getattr(self, "_bar_pos", {e: 0 for e in self.ENGS})
        for e in self.ENGS:
            lastc = None
            for op in reversed(self.ops[e][b0[e]:]):
                if op.dma:
                    deps.add((e, op.idx))
                elif lastc is None:
                    lastc = op
                    deps.add((e, op.idx))
        for e in self.ENGS:
            idx = len(self.ops[e])
            op = _Op(e, idx, lambda eng: eng.nop(), set(deps), False)
            self.ops[e].append(op)
        self._bar_pos = {e: len(self.ops[e]) for e in self.ENGS}
        self.last_writer = {}
        self.readers = {}

    def emit(self):
        self.flush(final=True)

    def flush(self, final=False):
        nc = self.nc
        ops = self.ops
        if not hasattr(self, "_pos"):
            self._pos = {e: 0 for e in self.ENGS}
            self._nms = {e: 0 for e in self.ENGS}
            self.comp_sems = {e: [] for e in self.ENGS}
            self._dpool = {}
            self._known = {e: {} for e in self.ENGS}
            self._ndma = {e: 0 for e in self.ENGS}
        seg = {e: ops[e][self._pos[e]:] for e in self.ENGS}
        for e in self.ENGS:
            for op in seg[e]:
                for (de, di) in op.deps:
                    ops[de][di].needs_inc = True
        for e in self.ENGS:
            for op in seg[e]:
                if op.dma:
                    continue
                if op.needs_inc:
                    chunk, off = divmod(self._nms[e], SEM_CHUNK)
                    while len(self.comp_sems[e]) <= chunk:
                        self.comp_sems[e].append(nc.alloc_semaphore(name=f"s_{e}_{len(self.comp_sems[e])}"))
                    op.ms = (chunk, off + 1)
                    self._nms[e] += 1
        POOL = {"sp": 20, "pool": 10, "act": 2, "pe": 1, "dve": 1}
        for e in self.ENGS:
            for k, op in enumerate(seg[e]):
                if op.cc:
                    op.sem = nc.alloc_semaphore(name=f"cc_{e}_{self._pos[e] + k}")
                    op.prev_target = 0
                    op.target = 1
            dmas = [op for op in seg[e] if op.dma and not op.cc]
            if not dmas:
                continue
            if e not in self._dpool:
                self._dpool[e] = [[nc.alloc_semaphore(name=f"d_{e}_{i}"), 0] for i in range(POOL[e])]
            pool = self._dpool[e]
            for op in dmas:
                k = self._ndma[e]
                self._ndma[e] += 1
                slot = pool[k % len(pool)]
                if slot[1] + 16 > DMA_SEM_MAX:
                    slot[0] = nc.alloc_semaphore(name=f"d_{e}_x{k}")
                    slot[1] = 0
                op.sem = slot[0]
                op.prev_target = slot[1]
                slot[1] += 16
                op.target = slot[1]
        engmap = {"pe": "tensor", "act": "scalar", "dve": "vector", "pool": "gpsimd", "sp": "sync"}
        with nc.Block() as block:
            for e in self.ENGS:
                if seg[e] or final:
                    getattr(block, engmap[e])(self._make_emitter(e, seg[e], final))
        for e in self.ENGS:
            self._pos[e] = len(ops[e])

    def _make_emitter(self, e, seg, final):
        ops = self.ops
        comp_sems = self.comp_sems
        known = self._known[e]

        def emitter(eng):
            def wait(sem, val):
                kid = id(sem)
                if known.get(kid, 0) >= val:
                    return
                eng.wait_ge(sem, val)
                known[kid] = val

            for op in seg:
                if op.deps:
                    need = {}
                    for (de, di) in op.deps:
                        p = ops[de][di]
                        if p.dma:
                            sem, val = p.sem, p.target
                        else:
                            sem, val = comp_sems[de][p.ms[0]], p.ms[1]
                        kid = id(sem)
                        if kid not in need or need[kid][1] < val:
                            need[kid] = (sem, val)
                    for sem, val in need.values():
                        wait(sem, val)
                if op.cc:
                    ins = op.fn(eng)
                    ins.then_inc(op.sem)
                elif op.dma:
                    if op.prev_target:
                        wait(op.sem, op.prev_target)
                    ins = op.fn(eng)
                    ins.then_inc(op.sem, 16)
                else:
                    ins = op.fn(eng)
                    if op.needs_inc:
                        ins.then_inc(comp_sems[e][op.ms[0]], 1)
            if final:
                last = {}
                for op in ops[e]:
                    if op.dma:
                        last[id(op.sem)] = (op.sem, op.target)
                for sem, tgt in last.values():
                    wait(sem, tgt)

        return emitter


class Ring:
    def __init__(self, items):
        self.items = list(items)
        self.i = 0

    def next(self):
        it = self.items[self.i % len(self.items)]
        self.i += 1
        return it


class WStream:
    def __init__(self, P, nc, nbuf, cols, name, st):
        self.P = P
        self.nbuf = nbuf
        self.name = st.name(name)
        self.bufs = [st.alloc(f"{name}{i}", [128, cols], BF16) for i in range(nbuf)]
        self.i = 0

    def load(self, src, ncols, parts=128):
        b = self.i % self.nbuf
        self.i += 1
        self.P.dma(self.bufs[b][0:parts, 0:ncols], src, writes=[(self.name, b)], q="pool")
        return b


def run_jobs(ws, jobs):
    loaded = {}
    nxt = 0
    outstanding = 0
    n = len(jobs)
    for i in range(n):
        while nxt < n and (outstanding < ws.nbuf or jobs[nxt][0] is None):
            if jobs[nxt][0] is not None:
                loaded[nxt] = jobs[nxt][0](ws)
                outstanding += 1
            nxt += 1
            if nxt > i and outstanding >= ws.nbuf:
                break
        b = loaded.pop(i, None)
        jobs[i][1](b)
        if jobs[i][0] is not None:
            outstanding -= 1


class Env:
    pass


def make_env(nc, P):
    env = Env()
    env.nc = nc
    env.P = P
    env.pb = [nc.alloc_psum_tensor(f"pb{i}", [128, 512], F32) for i in range(8)]
    env.ones_f = nc.alloc_sbuf_tensor("ones_f", [128, 128], F32)
    env.ones_b = nc.alloc_sbuf_tensor("ones_b", [128, 128], BF16)
    env.ident_f = nc.alloc_sbuf_tensor("ident_f", [128, 128], F32)
    env.ident_b = nc.alloc_sbuf_tensor("ident_b", [128, 128], BF16)
    P.add("dve", lambda e: e.memset(env.ones_f[:], 1.0), writes=["ones_f"])
    P.add("dve", lambda e: e.memset(env.ones_b[:], 1.0), writes=["ones_b"])
    return env


def load_ident(env, ident_ap):
    env.P.dma(env.ident_f[:], ident_ap, writes=["ident_f"])
    env.P.dma(env.ident_b[:], ident_ap, writes=["ident_b"], q="pool")


def load_const(env, name, dram_ap, shape, dtype=F32, q=None, st=None):
    t = st.alloc(name, shape, dtype) if st is not None else env.nc.alloc_sbuf_tensor(name, shape, dtype)
    if st is not None:
        name = st.name(name)
    if q is None:
        q = "pool" if dtype != F32 else "sp"
    env.P.dma(t[:], dram_ap, writes=[name], q=q)
    return t


def mm(P, out, lhsT, rhs, start, stop, reads, writes, **kw):
    P.add("pe", lambda e: e.matmul(out, lhsT=lhsT, rhs=rhs, start=start, stop=stop, **kw), reads=reads, writes=writes)


def rowlocal_phase(env, mode, io, st):
    nc, P, pb = env.nc, env.P, env.pb
    acc = Ring([0, 1, 2, 3])
    tpb = Ring([5, 6])
    SSB = 4
    MISC = 7

    hT = st.alloc("hT", [128, 16, TT], F32)
    xnT = st.alloc("xnT", [128, 16, TT], BF16)
    OT = st.alloc("OT", [128, 16, TT], BF16)
    aT = st.alloc("aT", [128, NF, TT], BF16)
    sqs = [st.alloc(f"sq{i}", [128, TT], F32) for i in range(3)]
    sqr = Ring(range(3))
    rs = st.alloc("rs", [128, TT], F32)
    rs2 = st.alloc("rs2", [128, TT], F32)
    ws = WStream(P, nc, 3, NF * 128, "wr", st)
    g_ffn = load_const(env, "g_ffn", io["g_ffn"], [128, 16], st=st)
    if "ot_gview" in io:
        gidx = load_const(env, "gidx", io["gidx"], [128, 64], mybir.dt.int32, q="sp", st=st)
    ident_f = env.ident_f
    if mode == "B":
        xs = [st.alloc(f"xs{i}", [128, D], F32) for i in range(2)]
        g_attn1 = load_const(env, "g_attn1", io["g_attn1"], [128, 16], st=st)
        g_kv = load_const(env, "g_kv", io["g_kv"], [128, 16], st=st)
        g_qa = load_const(env, "g_qa", io["g_qa"], [128, 4], st=st)
        g_kva = load_const(env, "g_kva", io["g_kva"], [128, 4], st=st)
        rm64 = load_const(env, "rm64", io["rm64"], [64, 64], BF16, st=st)
        craw = st.alloc("craw", [128, 4, TT], F32)
        cst = st.alloc("cst", [128, 4, TT], BF16)
        kraw = st.alloc("kraw", [64, TT], F32)
        kraw_b = st.alloc("kraw_b", [64, TT], BF16)
        kt1 = st.alloc("kt1", [64, TT], F32)
        kt2 = st.alloc("kt2", [64, TT], F32)
        kout = st.alloc("kout", [64, TT], BF16)
        cosT = st.alloc("cosT", [64, TT], F32)
        sinT = st.alloc("sinT", [64, TT], F32)
    else:
        g_fin = load_const(env, "g_fin", io["g_fin"], [128, 16], st=st)
        osb = [st.alloc(f"osb{i}", [128, D], F32) for i in range(2)]
    sgs = [st.alloc(f"sg{i}", [128, TT], F32) for i in range(2)]
    sgr = Ring(range(2))

    def hk(k):
        return ("hT", k)

    def norm_stats(src_fn, nchunk, keys, dim, dst, dstkey):
        for k in range(nchunk):
            r = sqr.next()
            P.add("act", lambda e, k=k, r=r: e.activation(out=sqs[r][:], in_=src_fn(k), func=AF.Square),
                  reads=[keys(k)], writes=[("sq", r)])
            mm(P, pb[SSB][:], env.ones_f[:], sqs[r][:], k == 0, k == nchunk - 1,
               reads=[("sq", r), "ones_f"], writes=[("pb", SSB)])
        P.add("act", lambda e: e.activation(out=dst[:], in_=pb[SSB][:], func=AF.Sqrt, scale=1.0 / dim, bias=EPS),
              reads=[("pb", SSB)], writes=[dstkey])
        P.add("dve", lambda e: e.reciprocal(out=dst[:], in_=dst[:]), reads=[dstkey], writes=[dstkey])

    def apply_norm(g, rsbuf, rskey):
        for k in range(16):
            P.add("dve", lambda e, k=k: e.scalar_tensor_tensor(out=xnT[:, k, :], in0=hT[:, k, :], scalar=g[:, k:k + 1],
                                                               in1=rsbuf[:], op0=ALU.mult, op1=ALU.mult),
                  reads=[hk(k), rskey], writes=[("xnT", k)])

    jobs = []
    for t in range(TOK // TT):
        t0 = t * TT

        def init_res(_b, t=t, t0=t0):
            if mode == "B":
                for sub in range(4):
                    xb = sub % 2
                    P.dma(xs[xb][:], io["res_in"][t0 + sub * 128:t0 + (sub + 1) * 128, :], writes=[("xs", xb)])
                    for grp in range(4):
                        bk = tpb.next()
                        for j in range(4):
                            k = grp * 4 + j
                            P.add("pe", lambda e, k=k, j=j, bk=bk, xb=xb: e.transpose(
                                pb[bk][:, j * 128:(j + 1) * 128], xs[xb][:, k * 128:(k + 1) * 128], ident_f[:]),
                                reads=[("xs", xb), "ident_f"], writes=[("pb", bk)])
                        P.add("act", lambda e, grp=grp, bk=bk, sub=sub: e.activation(
                            out=hT[:, grp * 4:grp * 4 + 4, sub * 128:(sub + 1) * 128],
                            in_=pb[bk][:].rearrange("p (a b) -> p a b", b=128), func=AF.Copy),
                            reads=[("pb", bk)], writes=[hk(grp * 4 + j) for j in range(4)])
            else:
                P.dma(hT[:], io["res_in"][t], writes=[hk(k) for k in range(16)])
            if "ot_gview" in io:
                for ih in range(16):
                    col = ih * 4 + t
                    P.add("pool", lambda e, ih=ih, col=col: e.indirect_dma_start(
                        out=OT[:, ih, :], out_offset=None, in_=io["ot_gview"],
                        in_offset=bass.IndirectOffsetOnAxis(ap=gidx[:, col:col + 1], axis=0)),
                        reads=[io["ot_key"], st.name("gidx")], writes=[("OT", ih % 2)], dma=True)
            else:
                for hh in range(2):
                    P.dma(OT[:].rearrange("p (i h) t -> p i h t", h=2)[:, :, hh, :],
                          io["ot_in"][:, :, hh, t0:t0 + TT].rearrange("i p t -> p i t"), writes=[("OT", hh)])
        jobs.append((None, init_res))

        for dt in range(16):
            def ld(w, dt=dt):
                return w.load(io["wo_t"][dt], 16 * 128)

            def cp(b, dt=dt):
                wv = ws.bufs[b][:, 0:16 * 128].rearrange("p (h j) -> p h j", j=128)
                bk = acc.next()
                for h in range(16):
                    mm(P, pb[bk][:], wv[:, h, :], OT[:, h, :], h == 0, h == 15,
                       reads=[(ws.name, b), ("OT", 0), ("OT", 1)], writes=[("pb", bk)])
                P.add("dve", lambda e: e.tensor_tensor(out=hT[:, dt, :], in0=hT[:, dt, :], in1=pb[bk][:], op=ALU.add),
                      reads=[hk(dt), ("pb", bk)], writes=[hk(dt)])
            jobs.append((ld, cp))

        def ffn_norm(_b):
            norm_stats(lambda k: hT[:, k, :], 16, hk, D, rs, "rs")
            apply_norm(g_ffn, rs, "rs")
        jobs.append((None, ffn_norm))
        for ft in range(NF):
            def ld(w, ft=ft):
                return w.load(io["wgu_t"][ft], 2 * 16 * 128)

            def cp(b, ft=ft):
                wv = ws.bufs[b][:, 0:2 * 16 * 128].rearrange("p (g k j) -> p g k j", g=2, j=128)
                bg = acc.next()
                bu = acc.next()
                for k in range(16):
                    mm(P, pb[bg][:], wv[:, 0, k, :], xnT[:, k, :], k == 0, k == 15,
                       reads=[(ws.name, b), ("xnT", k)], writes=[("pb", bg)])
                for k in range(16):
                    mm(P, pb[bu][:], wv[:, 1, k, :], xnT[:, k, :], k == 0, k == 15,
                       reads=[(ws.name, b), ("xnT", k)], writes=[("pb", bu)])
                r = sgr.next()
                P.add("act", lambda e: e.activation(out=sgs[r][:], in_=pb[bg][:], func=AF.Silu),
                      reads=[("pb", bg)], writes=[("sg", r)])
                P.add("dve", lambda e: e.tensor_tensor(out=aT[:, ft, :], in0=sgs[r][:], in1=pb[bu][:], op=ALU.mult),
                      reads=[("sg", r), ("pb", bu)], writes=[("aT", ft)])
            jobs.append((ld, cp))
        for dt in range(16):
            def ld(w, dt=dt):
                return w.load(io["wdn_t"][dt], NF * 128)

            def cp(b, dt=dt):
                wv = ws.bufs[b][:, 0:NF * 128].rearrange("p (f j) -> p f j", j=128)
                bk = acc.next()
                for f in range(NF):
                    mm(P, pb[bk][:], wv[:, f, :], aT[:, f, :], f == 0, f == NF - 1,
                       reads=[(ws.name, b), ("aT", f)], writes=[("pb", bk)])
                P.add("dve", lambda e: e.tensor_tensor(out=hT[:, dt, :], in0=hT[:, dt, :], in1=pb[bk][:], op=ALU.add),
                      reads=[hk(dt), ("pb", bk)], writes=[hk(dt)])
            jobs.append((ld, cp))

        if mode == "B":
            def pre1(_b, t=t):
                P.dma(io["h1T_out"][t], hT[:], reads=[hk(k) for k in range(16)], writes=[("h1T_out", t)])
                norm_stats(lambda k: hT[:, k, :], 16, hk, D, rs, "rs")
                apply_norm(g_attn1, rs, "rs")
            jobs.append((None, pre1))

            def lat_jobs(wkey, gq, seg0, t0=t0):
                for qt in range(4):
                    def ld(w, qt=qt):
                        return w.load(io[wkey][qt], 16 * 128)

                    def cp(b, qt=qt):
                        wv = ws.bufs[b][:, 0:16 * 128].rearrange("p (k j) -> p k j", j=128)
                        bk = acc.next()
                        for k in range(16):
                            mm(P, pb[bk][:], wv[:, k, :], xnT[:, k, :], k == 0, k == 15,
                               reads=[(ws.name, b), ("xnT", k)], writes=[("pb", bk)])
                        P.add("act", lambda e: e.activation(out=craw[:, qt, :], in_=pb[bk][:], func=AF.Copy),
                              reads=[("pb", bk)], writes=[("craw", qt)])
                    jobs.append((ld, cp))

                def fin(_b):
                    norm_stats(lambda k: craw[:, k, :], 4, lambda k: ("craw", k), 512, rs2, "rs2")
                    for qt in range(4):
                        P.add("dve", lambda e, qt=qt: e.scalar_tensor_tensor(
                            out=cst[:, qt, :], in0=craw[:, qt, :], scalar=gq[:, qt:qt + 1], in1=rs2[:],
                            op0=ALU.mult, op1=ALU.mult), reads=[("craw", qt), "rs2"], writes=[("cst", qt)])
                    P.dma(io["ag_in"][:, seg0:seg0 + 4, t0:t0 + TT], cst[:],
                          reads=[("cst", q) for q in range(4)], writes=[("ag_in", seg0, t0)])
                jobs.append((None, fin))
            lat_jobs("wqa_t", g_qa, 0)

            def pre2(_b):
                apply_norm(g_kv, rs, "rs")
            jobs.append((None, pre2))
            lat_jobs("wkva_t", g_kva, 4)

            def ldk(w):
                return w.load(io["wkpe_t"], 16 * 64)

            def cpk(b, t=t, t0=t0):
                wv = ws.bufs[b][:, 0:16 * 64].rearrange("p (k j) -> p k j", j=64)
                bk = acc.next()
                for k in range(16):
                    mm(P, pb[bk][0:64, :], wv[:, k, :], xnT[:, k, :], k == 0, k == 15,
                       reads=[(ws.name, b), ("xnT", k)], writes=[("pb", bk)])
                P.dma(cosT[:], io["cosB"][t], writes=["cosT"])
                P.dma(sinT[:], io["sinB"][t], writes=["sinT"])
                P.add("act", lambda e: e.activation(out=kraw[:], in_=pb[bk][0:64, :], func=AF.Copy),
                      reads=[("pb", bk)], writes=["kraw"])
                P.add("act", lambda e: e.activation(out=kraw_b[:], in_=pb[bk][0:64, :], func=AF.Copy),
                      reads=[("pb", bk)], writes=["kraw_b"])
                mm(P, pb[MISC][0:64, :], rm64[:], kraw_b[:], True, True, reads=["kraw_b", st.name("rm64")],
                   writes=[("pb", MISC)])
                P.add("dve", lambda e: e.tensor_tensor(out=kt1[:], in0=kraw[:], in1=cosT[:], op=ALU.mult),
                      reads=["kraw", "cosT"], writes=["kt1"])
                P.add("dve", lambda e: e.tensor_tensor(out=kt2[:], in0=sinT[:], in1=pb[MISC][0:64, :], op=ALU.mult),
                      reads=["sinT", ("pb", MISC)], writes=["kt2"])
                P.add("dve", lambda e: e.tensor_tensor(out=kout[:], in0=kt1[:], in1=kt2[:], op=ALU.add),
                      reads=["kt1", "kt2"], writes=["kout"])
                P.dma(io["ag_in"][0:64, 8, t0:t0 + TT], kout[:], reads=["kout"], writes=[("ag_in", 8, t0)])
            jobs.append((ldk, cpk))
        else:
            def fin(_b, t0=t0):
                norm_stats(lambda k: hT[:, k, :], 16, hk, D, rs, "rs")
                for k in range(16):
                    P.add("dve", lambda e, k=k: e.scalar_tensor_tensor(
                        out=hT[:, k, :], in0=hT[:, k, :], scalar=g_fin[:, k:k + 1], in1=rs[:],
                        op0=ALU.mult, op1=ALU.mult), reads=[hk(k), "rs"], writes=[hk(k)])
                for sub in range(4):
                    ob = sub % 2
                    for grp in range(4):
                        bk = tpb.next()
                        for j in range(4):
                            k = grp * 4 + j
                            P.add("pe", lambda e, k=k, j=j, bk=bk, sub=sub: e.transpose(
                                pb[bk][:, j * 128:(j + 1) * 128], hT[:, k, sub * 128:(sub + 1) * 128], ident_f[:]),
                                reads=[hk(k), "ident_f"], writes=[("pb", bk)])
                        P.add("act", lambda e, grp=grp, bk=bk, ob=ob: e.activation(
                            out=osb[ob][:, grp * 512:(grp + 1) * 512], in_=pb[bk][:], func=AF.Copy),
                            reads=[("pb", bk)], writes=[("osb", ob)])
                    P.dma(io["out"][t0 + sub * 128:t0 + (sub + 1) * 128, :], osb[ob][:],
                          reads=[("osb", ob)], writes=[("out", t0, sub)])
            jobs.append((None, fin))
    run_jobs(ws, jobs)


class NameScope:
    def __init__(self, prefix, nc=None, stack=None):
        self.prefix = prefix
        self.nc = nc
        self.stack = stack

    def enter_name(self, n):
        return f"{self.prefix}_{n}"

    def name(self, n):
        return f"{self.prefix}_{n}"

    def alloc(self, n, shape, dt):
        if self.stack is None:
            return self.nc.alloc_sbuf_tensor("sb" + self.name(n), shape, dt)
        return self.stack.enter_context(self.nc.sbuf_tensor("sb" + self.name(n), shape, dt))


def mla_phase(env, io, st):
    nc, P, pb = env.nc, env.P, env.pb
    SC = (128 + 64) ** -0.5
    sbank = Ring([0, 1, 2])
    prj = Ring([3, 4])
    OB, DB = 5, 6
    MISC = 7
    KnT = st.alloc("KnT", [128, S], BF16)
    KpT = st.alloc("KpT", [64, S], BF16)
    V = st.alloc("V", [128, S // 128, 128], BF16)
    cqT = [st.alloc(f"cqT{i}", [128, 4, TT], BF16) for i in range(2)]
    cT = [st.alloc(f"cT{i}", [128, 4, TT], BF16) for i in range(2)]
    QnT = [st.alloc(f"QnT{i}", [128, TT], BF16) for i in range(2)]
    QpT = [st.alloc(f"QpT{i}", [64, TT], BF16) for i in range(2)]
    qraw = st.alloc("qraw", [64, TT], F32)
    qraw_b = st.alloc("qraw_b", [64, TT], BF16)
    qt1 = st.alloc("qt1", [64, TT], F32)
    qt2 = st.alloc("qt2", [64, TT], F32)
    cosq = [st.alloc(f"cosq{i}", [64, TT], F32) for i in range(2)]
    sinq = [st.alloc(f"sinq{i}", [64, TT], F32) for i in range(2)]
    PT = [st.alloc(f"PT{i}", [128, TT], BF16) for i in range(4)]
    ptr = Ring(range(4))
    rden = st.alloc("rden", [128, TT], F32)
    Oout = [st.alloc(f"Oout{i}", [128, TT], BF16) for i in range(2)]
    wqb = load_const(env, "wqb", io["wqb_t"], [128, 2, 4 * 192], BF16, st=st)
    wkvb = load_const(env, "wkvb", io["wkvb_t"], [128, 2, 4 * 256], BF16, st=st)
    rm64 = load_const(env, "rm64", io["rm64"], [64, 64], BF16, st=st)
    tri = load_const(env, "tri", io["tri"], [128, 128], BF16, st=st)
    NB = S // TT

    for hl in range(2):
        wq = wqb[:, hl, :].rearrange("p (k j) -> p k j", j=192)
        wk = wkvb[:, hl, :].rearrange("p (k j) -> p k j", j=256)
        for b in range(NB):
            r, off = divmod(b * TT, TOK)
            cb = b % 2
            P.dma(cqT[cb][:], io["ag_out"][r, :, 0:4, off:off + TT], reads=["ag_out"], writes=[("cqT", cb)])
            P.dma(cT[cb][:], io["ag_out"][r, :, 4:8, off:off + TT], reads=["ag_out"], writes=[("cT", cb)])
            P.dma(cosq[cb][:], io["cosB_all"][:, b * TT:(b + 1) * TT], writes=[("cosq", cb)])
            P.dma(sinq[cb][:], io["sinB_all"][:, b * TT:(b + 1) * TT], writes=[("sinq", cb)])
            if hl == 0:
                P.dma(KpT[:, b * TT:(b + 1) * TT], io["ag_out"][r, 0:64, 8, off:off + TT], reads=["ag_out"], writes=[("KpT", b)])
            bk = prj.next()
            for k in range(4):
                mm(P, pb[bk][:], wq[:, k, 0:128], cqT[cb][:, k, :], k == 0, k == 3,
                   reads=[("cqT", cb), st.name("wqb")], writes=[("pb", bk)])
            P.add("act", lambda e, bk=bk, cb=cb: e.activation(out=QnT[cb][:], in_=pb[bk][:], func=AF.Copy),
                  reads=[("pb", bk)], writes=[("QnT", cb)])
            bk = prj.next()
            for k in range(4):
                mm(P, pb[bk][0:64, :], wq[:, k, 128:192], cqT[cb][:, k, :], k == 0, k == 3,
                   reads=[("cqT", cb), st.name("wqb")], writes=[("pb", bk)])
            P.add("act", lambda e, bk=bk: e.activation(out=qraw[:], in_=pb[bk][0:64, :], func=AF.Copy),
                  reads=[("pb", bk)], writes=["qraw"])
            P.add("act", lambda e, bk=bk: e.activation(out=qraw_b[:], in_=pb[bk][0:64, :], func=AF.Copy),
                  reads=[("pb", bk)], writes=["qraw_b"])
            mm(P, pb[MISC][0:64, :], rm64[:], qraw_b[:], True, True, reads=["qraw_b", st.name("rm64")],
               writes=[("pb", MISC)])
            P.add("dve", lambda e, cb=cb: e.tensor_tensor(out=qt1[:], in0=qraw[:], in1=cosq[cb][:], op=ALU.mult),
                  reads=["qraw", ("cosq", cb)], writes=["qt1"])
            P.add("dve", lambda e, cb=cb: e.tensor_tensor(out=qt2[:], in0=sinq[cb][:], in1=pb[MISC][0:64, :], op=ALU.mult),
                  reads=[("sinq", cb), ("pb", MISC)], writes=["qt2"])
            P.add("dve", lambda e, cb=cb: e.tensor_tensor(out=QpT[cb][:], in0=qt1[:], in1=qt2[:], op=ALU.add),
                  reads=["qt1", "qt2"], writes=[("QpT", cb)])
            bk = prj.next()
            for k in range(4):
                mm(P, pb[bk][:], wk[:, k, 0:128], cT[cb][:, k, :], k == 0, k == 3,
                   reads=[("cT", cb), st.name("wkvb")], writes=[("pb", bk)])
            P.add("act", lambda e, bk=bk, b=b: e.activation(out=KnT[:, b * TT:(b + 1) * TT], in_=pb[bk][:], func=AF.Copy),
                  reads=[("pb", bk)], writes=[("KnT", b)])
            bk = prj.next()
            for sub in range(4):
                for k in range(4):
                    mm(P, pb[bk][:, sub * 128:(sub + 1) * 128], cT[cb][:, k, sub * 128:(sub + 1) * 128], wk[:, k, 128:256],
                       sub == 0 and k == 0, k == 3, reads=[("cT", cb), st.name("wkvb")], writes=[("pb", bk)],
                       skip_group_check=True)
            P.add("act", lambda e, bk=bk, b=b: e.activation(
                out=V[:, 4 * b:4 * b + 4, :], in_=pb[bk][:].rearrange("p (a j) -> p a j", j=128), func=AF.Copy),
                reads=[("pb", bk)], writes=[("V", b)])
            nch = 4 * b + 4
            SK = 2
            pend = []

            def s_step(c, b=b, cb=cb):
                j = c - 4 * b
                lo = 128 * j if j > 0 else 0
                sb = sbank.next()
                mm(P, pb[sb][:, lo:TT], KnT[:, c * 128:(c + 1) * 128], QnT[cb][:, lo:TT], True, False,
                   reads=[("KnT", c // 4), ("QnT", cb)], writes=[("pb", sb)])
                mm(P, pb[sb][:, lo:TT], KpT[:, c * 128:(c + 1) * 128], QpT[cb][:, lo:TT], False, True,
                   reads=[("KpT", c // 4), ("QpT", cb)], writes=[("pb", sb)])
                pr = ptr.next()
                P.add("act", lambda e: e.activation(out=PT[pr][:, lo:TT], in_=pb[sb][:, lo:TT], func=AF.Exp, scale=SC),
                      reads=[("pb", sb)], writes=[("PT", pr)])
                if j >= 0:
                    P.add("dve", lambda e: e.tensor_tensor(out=PT[pr][:, lo:lo + 128], in0=PT[pr][:, lo:lo + 128],
                                                           in1=tri[:], op=ALU.mult),
                          reads=[("PT", pr), st.name("tri")], writes=[("PT", pr)])
                return (c, pr, lo)

            def pv_step(item, nch=nch):
                c, pr, lo = item
                mm(P, pb[OB][:, lo:TT], V[:, c, :], PT[pr][:, lo:TT], c == 0, c == nch - 1,
                   reads=[("V", c // 4), ("PT", pr)], writes=[("pb", OB)], skip_group_check=True)
                mm(P, pb[DB][:, lo:TT], env.ones_b[:], PT[pr][:, lo:TT], c == 0, c == nch - 1,
                   reads=["ones_b", ("PT", pr)], writes=[("pb", DB)], skip_group_check=True)

            for c in range(nch):
                pend.append(s_step(c))
                if len(pend) > SK:
                    pv_step(pend.pop(0))
            while pend:
                pv_step(pend.pop(0))
            ob = b % 2
            P.add("dve", lambda e: e.reciprocal(out=rden[:], in_=pb[DB][:]), reads=[("pb", DB)], writes=["rden"])
            P.add("dve", lambda e, ob=ob: e.tensor_tensor(out=Oout[ob][:], in0=rden[:], in1=pb[OB][:], op=ALU.mult),
                  reads=["rden", ("pb", OB)], writes=[("Oout", ob)])
            P.dma(io["o_out"][r, :, hl, off:off + TT], Oout[ob][:], reads=[("Oout", ob)], writes=[("o_out", b, hl)])


A_GROUPS = ((128, 1), (512, 4), (2048, 16))
DBG = {}
CH = 2048


def act(P, out, in_, func, reads, writes, **kw):
    P.add("act", lambda e: e.activation(out=out, in_=in_, func=func, **kw), reads=reads, writes=writes)


def tt(P, out, in0, in1, op, reads, writes, eng="dve"):
    P.add(eng, lambda e: e.tensor_tensor(out=out, in0=in0, in1=in1, op=op), reads=reads, writes=writes)


def tr(P, out, in_, ident, reads, writes):
    P.add("pe", lambda e: e.transpose(out, in_, ident), reads=reads, writes=writes)


def dil_phase(env, io, st):
    nc, P, pb = env.nc, env.P, env.pb
    SC = 128 ** -0.5
    acc = Ring([0, 1, 2])
    tbr = Ring([6, 7])
    SB = Ring([3, 4, 5])
    xs = [st.alloc(f"xs{i}", [128, D], F32) for i in range(2)]
    junk = st.alloc("junk", [128, D], BF16)
    xb = [st.alloc(f"xb{i}", [128, D], BF16) for i in range(2)]
    ssq = [st.alloc(f"ssq{i}", [128, 4], F32) for i in range(4)]
    ssr = Ring(range(4))
    xnT = [st.alloc(f"xnT{i}", [128, 16, TT], BF16) for i in range(2)]
    ws = WStream(P, nc, 3, 16 * 128, "wa", st)
    gA = load_const(env, "gA", io["g_attn0"], [128, 16], st=st)
    rm32 = load_const(env, "rm32", io["rm32"], [32, 32], BF16, st=st)
    masks = load_const(env, "masks", io["masks"], [128, 3, 512], BF16, st=st)
    QT = [st.alloc(f"QT{g}", [128, CH], BF16) for g in range(3)]
    KT = [st.alloc(f"KT{g}", [128, 2, CH], BF16) for g in range(3)]
    VT = [st.alloc(f"VT{g}", [128, CH], BF16) for g in range(3)]
    VK = [st.alloc(f"VK{g}", [128, 2, 16, 128], BF16) for g in range(3)]
    raw = st.alloc("raw", [32, TT], BF16)
    rt1 = st.alloc("rt1", [32, TT], F32)
    rt2 = st.alloc("rt2", [32, TT], F32)
    cosA = [st.alloc(f"cosA{i}", [32, TT], F32) for i in range(2)]
    sinA = [st.alloc(f"sinA{i}", [32, TT], F32) for i in range(2)]
    PT = [st.alloc(f"PT{i}", [128, 512], BF16) for i in range(3)]
    ptr = Ring(range(3))
    accO = st.alloc("accO", [128, CH], F32)
    accD = st.alloc("accD", [128, CH], F32)
    Oout = st.alloc("Oout", [128, CH], BF16)
    for g in range(3):
        P.add("pool", lambda e, g=g: e.memset(KT[g][:], 0.0), writes=[("KT", g, 0), ("KT", g, 1)])
        P.add("pool", lambda e, g=g: e.memset(VK[g][:], 0.0), writes=[("VK", g, 0), ("VK", g, 1)])
    NCH = S // CH
    ident_b = env.ident_b
    gkey = st.name("gA")

    def norm_tile(p0, xi):
        P.dma(cosA[xi][:], io["cosA"][:, p0:p0 + TT], writes=[("cosA", xi)])
        P.dma(sinA[xi][:], io["sinA"][:, p0:p0 + TT], writes=[("sinA", xi)])
        for sub in range(4):
            sx = sub % 2
            sq = ssq[ssr.next()]
            sk = ("ssq", sq.name if hasattr(sq, "name") else id(sq))
            P.dma(xs[sx][:], io["x"][p0 + sub * 128:p0 + (sub + 1) * 128, :], writes=[("xs", sx)])
            P.add("dve", lambda e, sq=sq: e.memset(sq[:, 0:1], 0.0), writes=[sk])
            act(P, junk[:], xs[sx][:], AF.Square, reads=[("xs", sx), sk], writes=["junk", sk], accum_out=sq[:, 0:1])
            act(P, sq[:, 1:2], sq[:, 0:1], AF.Sqrt, reads=[sk], writes=[sk], scale=1.0 / D, bias=EPS)
            P.add("dve", lambda e, sq=sq: e.reciprocal(out=sq[:, 2:3], in_=sq[:, 1:2]), reads=[sk], writes=[sk])
            P.add("dve", lambda e, sq=sq, sx=sx: e.tensor_scalar(out=xb[sx][:], in0=xs[sx][:], scalar1=sq[:, 2:3],
                                                                 scalar2=None, op0=ALU.mult),
                  reads=[("xs", sx), sk], writes=[("xb", sx)])
            for half in range(2):
                bk = tbr.next()
                pv = pb[bk][:].bitcast(BF16)
                for j in range(8):
                    k = half * 8 + j
                    tr(P, pv[:, j * 128:(j + 1) * 128], xb[sx][:, k * 128:(k + 1) * 128], ident_b[:],
                       reads=[("xb", sx), "ident_b"], writes=[("pb", bk)])
                for j in range(8):
                    k = half * 8 + j
                    act(P, xnT[xi][:, k, sub * 128:(sub + 1) * 128], pv[:, j * 128:(j + 1) * 128], AF.Copy,
                        reads=[("pb", bk), gkey], writes=[("xnT", xi, k)], scale=gA[:, k:k + 1])

    def qkv_tile(hl, par, tt_, xi):
        jobs = []
        c0 = tt_ * TT
        for g in range(3):
            for which in range(3):
                def ld(w, g=g, which=which):
                    return w.load(io["wa_t"][hl, g, which], 16 * 128)

                def cp(b, g=g, which=which):
                    wv = ws.bufs[b][:, 0:16 * 128].rearrange("p (k j) -> p k j", j=128)
                    bk = acc.next()
                    for k in range(16):
                        mm(P, pb[bk][:], wv[:, k, :], xnT[xi][:, k, :], k == 0, k == 15,
                           reads=[(ws.name, b), ("xnT", xi, k)], writes=[("pb", bk)])
                    if which == 2:
                        act(P, VT[g][:, c0:c0 + TT], pb[bk][:], AF.Copy, reads=[("pb", bk)], writes=[("VT", g, tt_)])
                        return
                    if which == 0:
                        dst, dst32, dkey = QT[g][:, c0:c0 + TT], QT[g][0:32, c0:c0 + TT], ("QT", g, tt_)
                    else:
                        dst, dst32, dkey = KT[g][:, par, c0:c0 + TT], KT[g][0:32, par, c0:c0 + TT], ("KT", g, par)
                    act(P, dst, pb[bk][:], AF.Copy, reads=[("pb", bk)], writes=[dkey])
                    act(P, raw[:], pb[bk][0:32, :], AF.Copy, reads=[("pb", bk)], writes=["raw"])
                    rb = SB.next()
                    mm(P, pb[rb][0:32, :], rm32[:], raw[:], True, True, reads=["raw", st.name("rm32")], writes=[("pb", rb)])
                    tt(P, rt1[:], cosA[xi][:], pb[bk][0:32, :], ALU.mult, reads=[("cosA", xi), ("pb", bk)], writes=["rt1"])
                    tt(P, rt2[:], sinA[xi][:], pb[rb][0:32, :], ALU.mult, reads=[("sinA", xi), ("pb", rb)], writes=["rt2"])
                    tt(P, dst32, rt1[:], rt2[:], ALU.add, reads=["rt1", "rt2", dkey], writes=[dkey])
                jobs.append((ld, cp))
        run_jobs(ws, jobs)

    def prev_blocks(g, dil, par, n):
        if dil == 1:
            if n > 0:
                return KT[g][:, par, (n - 1) * 128:n * 128], VK[g][:, par, n - 1, :]
            return KT[g][:, 1 - par, CH - 128:CH], VK[g][:, 1 - par, 15, :]
        if dil == 4:
            jj, rr = divmod(n, 4)
            if jj > 0:
                return _blk(KT[g], par, (jj - 1) * 4 + rr, dil), VK[g][:, par, (jj - 1) * 4 + rr, :]
            return _blk(KT[g], 1 - par, 12 + rr, dil), VK[g][:, 1 - par, 12 + rr, :]
        return _blk(KT[g], 1 - par, n, dil), VK[g][:, 1 - par, n, :]

    def attn_chunk(hl, C, par):
        for g, (_win, dil) in enumerate(A_GROUPS):
            qkeys = [("QT", g, tq) for tq in range(4)]
            kkeys = [("KT", g, 0), ("KT", g, 1)]
            vkeys = [("VK", g, 0), ("VK", g, 1)]
            for q4 in range(4):
                bk = tbr.next()
                pv = pb[bk][:].bitcast(BF16)
                for j in range(4):
                    n = q4 * 4 + j
                    tr(P, pv[:, j * 128:(j + 1) * 128], _blk(VT[g], None, n, dil), ident_b[:],
                       reads=[("VT", g, tq) for tq in range(4)] + ["ident_b"], writes=[("pb", bk)])
                act(P, VK[g][:, par, q4 * 4:q4 * 4 + 4, :], pv[:, 0:512].rearrange("p (a j) -> p a j", j=128), AF.Copy,
                    reads=[("pb", bk)], writes=[("VK", g, par)])
            for q2 in range(8):
                sb = SB.next()
                vps = []
                for u in range(2):
                    n = q2 * 2 + u
                    qsrc = _blk(QT[g], None, n, dil)
                    kprev, vprev = prev_blocks(g, dil, par, n)
                    vps.append(vprev)
                    kcur = _blk(KT[g], par, n, dil)
                    mm(P, pb[sb][:, u * 256:u * 256 + 128], kprev, qsrc, True, True,
                       reads=kkeys + qkeys, writes=[("pb", sb)], skip_group_check=True)
                    mm(P, pb[sb][:, u * 256 + 128:u * 256 + 256], kcur, qsrc, True, True,
                       reads=kkeys + qkeys, writes=[("pb", sb)], skip_group_check=True)
                pr = ptr.next()
                act(P, PT[pr][:], pb[sb][:], AF.Exp, reads=[("pb", sb)], writes=[("PT", pr)], scale=SC)
                fm = _first_mask_sel(g, q2, C)
                tt(P, PT[pr][:], PT[pr][:], masks[:, fm, :], ALU.mult, reads=[("PT", pr), st.name("masks")],
                   writes=[("PT", pr)])
                ob = acc.next()
                db = acc.next()
                for u in range(2):
                    n = q2 * 2 + u
                    vcur = VK[g][:, par, n, :]
                    o_ap = pb[ob][:, u * 128:(u + 1) * 128]
                    d_ap = pb[db][:, u * 128:(u + 1) * 128]
                    rk = vkeys + [("PT", pr)]
                    mm(P, o_ap, vps[u], PT[pr][:, u * 256:u * 256 + 128], u == 0, False, reads=rk, writes=[("pb", ob)],
                       skip_group_check=True)
                    mm(P, o_ap, vcur, PT[pr][:, u * 256 + 128:u * 256 + 256], False, True, reads=rk, writes=[("pb", ob)],
                       skip_group_check=True)
                    mm(P, d_ap, env.ones_b[:], PT[pr][:, u * 256:u * 256 + 128], u == 0, False,
                       reads=["ones_b", ("PT", pr)], writes=[("pb", db)], skip_group_check=True)
                    mm(P, d_ap, env.ones_b[:], PT[pr][:, u * 256 + 128:u * 256 + 256], False, True,
                       reads=["ones_b", ("PT", pr)], writes=[("pb", db)], skip_group_check=True)
                dO = _acc_dst(accO, q2, dil)
                dD = _acc_dst(accD, q2, dil)
                src_o = pb[ob][:, 0:256].rearrange("p (u i) -> p u i", i=128)
                src_d = pb[db][:, 0:256].rearrange("p (u i) -> p u i", i=128)
                if g == 0:
                    act(P, dO, src_o, AF.Copy, reads=[("pb", ob)], writes=["accO"])
                    act(P, dD, src_d, AF.Copy, reads=[("pb", db)], writes=["accD"])
                else:
                    tt(P, dO, dO, src_o, ALU.add, reads=[("pb", ob), "accO"], writes=["accO"])
                    tt(P, dD, dD, src_d, ALU.add, reads=[("pb", db), "accD"], writes=["accD"])
        P.add("dve", lambda e: e.reciprocal(out=accD[:], in_=accD[:]), reads=["accD"], writes=["accD"])
        tt(P, Oout[:], accO[:], accD[:], ALU.mult, reads=["accO", "accD"], writes=["Oout"])
        P.dma(io["o_out"][C, :, hl, :], Oout[:], reads=["Oout"], writes=[("o_out", C, hl)])

    for hl in range(DBG.get("nhl", 2)):
        for C in range(DBG.get("nch", NCH)):
            par = C % 2
            for tt_ in range(CH // TT):
                t = C * (CH // TT) + tt_
                norm_tile(t * TT, t % 2)
                if DBG.get("qkv", True):
                    qkv_tile(hl, par, tt_, t % 2)
            if DBG.get("attn", True):
                attn_chunk(hl, C, par)
            else:
                P.dma(io["o_out"][C, :, hl, :], QT[0][:], reads=[("QT", 0, tq) for tq in range(4)], writes=[("o_out", C, hl)])


def _blk(t, par, n, dil):
    if dil == 1:
        lo, step = n * 128, 1
    elif dil == 4:
        jj, rr = divmod(n, 4)
        lo, step = jj * 512 + rr, 4
    else:
        lo, step = n, 16
    hi = lo + step * 127 + 1
    if par is None:
        return t[:, lo:hi:step]
    return t[:, par, lo:hi:step]


def _acc_dst(t, q2, dil):
    if dil == 1:
        return t[:, q2 * 256:(q2 + 1) * 256].rearrange("p (u i) -> p u i", i=128)
    if dil == 4:
        jj, r0 = divmod(q2 * 2, 4)
        return t[:, jj * 512:(jj + 1) * 512].rearrange("p (i r) -> p r i", r=4)[:, r0:r0 + 2, :]
    return t[:, :].rearrange("p (i r) -> p r i", r=16)[:, q2 * 2:q2 * 2 + 2, :]


def _first_mask_sel(g, q2, C):
    if C != 0:
        return 1
    if g == 0:
        return 2 if q2 == 0 else 1
    if g == 1:
        return 0 if q2 < 2 else 1
    return 0


def _tile_w(w, ncoltile=128):
    K, N = w.shape
    kc = K // 128
    nt = N // ncoltile
    return np.ascontiguousarray(w.reshape(kc, 128, nt, ncoltile).transpose(2, 1, 0, 3).reshape(nt, 128, kc * ncoltile))


def _gT(g):
    return np.ascontiguousarray(g.reshape(-1, 128).T)


def _rope_tables(dim, positions):
    inv_freq = ROPE_THETA ** (-np.arange(0, dim, 2, dtype=np.float32) / np.float32(dim))
    inv_freq = inv_freq.astype(np.float32)
    ang = positions.astype(np.float32)[:, None] * inv_freq[None, :]
    cos = np.cos(ang).astype(np.float32)
    sin = np.sin(ang).astype(np.float32)
    cosT = np.concatenate([cos, cos], axis=1).T
    sinT = np.concatenate([sin, sin], axis=1).T
    return np.ascontiguousarray(cosT), np.ascontiguousarray(sinT)


def _rot_mat(dim):
    half = dim // 2
    R = np.zeros((dim, dim), np.float32)
    for e in range(half):
        R[e + half, e] = -1.0
        R[e, e + half] = 1.0
    return R


def _masks():
    kk = np.arange(128)[:, None]
    qq = np.arange(128)[None, :]
    prev = (kk >= qq).astype(np.float32)
    cur = (kk <= qq).astype(np.float32)
    zero = np.zeros_like(prev)
    reg = np.concatenate([prev, cur, prev, cur], axis=1)
    first = np.concatenate([zero, cur, zero, cur], axis=1)
    half = np.concatenate([zero, cur, prev, cur], axis=1)
    return np.ascontiguousarray(np.stack([first, reg, half], axis=1))


NPDT = {F32: np.float32, BF16: ml_dtypes.bfloat16}


def launch(build_fn, in_maps, out_specs):
    nc = bass.Bass("TRN2", target_bir_lowering=False)
    io = {}
    for name, arr in in_maps[0].items():
        dt = BF16 if arr.dtype == ml_dtypes.bfloat16 else F32
        io[name] = nc.dram_tensor(name, list(arr.shape), dt, kind="ExternalInput").ap()
    for name, (shape, dt) in out_specs.items():
        io[name] = nc.dram_tensor(name, list(shape), dt, kind="ExternalOutput").ap()
    P = Prog(nc)
    env = make_env(nc, P)
    load_ident(env, io["ident"])
    build_fn(env, io)
    P.emit()
    if DBG.get("trace"):
        res = run_bass_kernel_spmd(nc, in_maps, core_ids=list(range(NCORES)), trace=True)
        print("exec_time_ns", res.exec_time_ns)
    else:
        res = run_bass_kernel_spmd(nc, in_maps, core_ids=list(range(NCORES)))
    return res.results


def prep_rowlocal_weights(w_o, w_gu, w_down):
    gt = _tile_w(w_gu[:, :FF])
    ut = _tile_w(w_gu[:, FF:])
    wgu_t = np.ascontiguousarray(np.stack([gt, ut], axis=2).reshape(NF, 128, 2 * 16 * 128))
    return {"wo_t": _tile_w(w_o), "wgu_t": wgu_t, "wdn_t": _tile_w(w_down)}


def prep_D(inp):
    wts = prep_rowlocal_weights(inp["b_w_o"][0], inp["ffn_w_gu"][1], inp["ffn_w_down"][1])
    common = dict(wts)
    common["g_ffn"] = _gT(inp["ffn_norm_g"][1])
    common["g_fin"] = _gT(inp["final_norm_g"])
    return [dict(common) for _ in range(NCORES)]


def prep_B(xfull, inp):
    wts = prep_rowlocal_weights(inp["a_w_o"][0], inp["ffn_w_gu"][0], inp["ffn_w_down"][0])
    common = dict(wts)
    common["wqa_t"] = _tile_w(inp["b_w_q_a"][0])
    common["wkva_t"] = _tile_w(inp["b_w_kv_a"][:, :512])
    common["wkpe_t"] = _tile_w(inp["b_w_kv_a"][:, 512:576], 64)[0]
    common["g_ffn"] = _gT(inp["ffn_norm_g"][0])
    common["g_attn1"] = _gT(inp["attn_norm_g"][1])
    common["g_kv"] = _gT(inp["kv_norm_g"])
    common["g_qa"] = _gT(inp["b_q_a_norm_g"][0])
    common["g_kva"] = _gT(inp["b_kv_a_norm_g"])
    common["rm64"] = _rot_mat(64)
    maps = []
    for c in range(NCORES):
        m = dict(common)
        m["res_in"] = np.ascontiguousarray(xfull[c * TOK:(c + 1) * TOK])
        cosT, sinT = _rope_tables(64, np.arange(c * TOK, (c + 1) * TOK))
        m["cosB"] = np.ascontiguousarray(cosT.reshape(64, 4, TT).transpose(1, 0, 2))
        m["sinB"] = np.ascontiguousarray(sinT.reshape(64, 4, TT).transpose(1, 0, 2))
        maps.append(m)
    return maps


def _tri():
    kk = np.arange(128)[:, None]
    qq = np.arange(128)[None, :]
    return (kk <= qq).astype(np.float32)


def prep_C(inp):
    cosT, sinT = _rope_tables(64, np.arange(S))
    common = {"rm64": _rot_mat(64), "tri": _tri(), "cosB_all": cosT, "sinB_all": sinT}
    wqb = inp["b_w_q_b"][0]
    wkvb = inp["b_w_kv_b"]
    maps = []
    for c in range(NCORES):
        m = dict(common)
        q = [wqb[:, h * 192:(h + 1) * 192].reshape(4, 128, 192).transpose(1, 0, 2).reshape(128, 768) for h in (2 * c, 2 * c + 1)]
        k = [wkvb[:, h * 256:(h + 1) * 256].reshape(4, 128, 256).transpose(1, 0, 2).reshape(128, 1024) for h in (2 * c, 2 * c + 1)]
        m["wqb_t"] = np.ascontiguousarray(np.stack(q, axis=1))
        m["wkvb_t"] = np.ascontiguousarray(np.stack(k, axis=1))
        maps.append(m)
    return maps


def prep_A(xfull, inp):
    wq = inp["a_w_qkv"][0]
    cosT, sinT = _rope_tables(32, np.arange(S))
    common = {"x": np.ascontiguousarray(xfull), "g_attn0": _gT(inp["attn_norm_g"][0]), "rm32": _rot_mat(32),
              "masks": _masks(), "cosA": cosT, "sinA": sinT}
    maps = []
    for c in range(NCORES):
        m = dict(common)
        wa = np.empty((2, 3, 3, 128, 16 * 128), np.float32)
        for hl in range(2):
            h = 2 * c + hl
            for g in range(3):
                for which in range(3):
                    c0 = g * 6144 + which * 2048 + h * 128
                    wa[hl, g, which] = _tile_w(wq[:, c0:c0 + 128])[0]
        m["wa_t"] = wa
        maps.append(m)
    return maps


def _gather_idx(c):
    idx = np.zeros((128, 64), np.int32)
    p = np.arange(128)
    for i in range(NCORES):
        for h in range(2):
            for tb in range(4):
                idx[:, (i * 2 + h) * 4 + tb] = (((i * NCORES + c) * 128 + p) * 2 + h) * 4 + tb
    return idx


def kernel(**inputs):
    inp = {k: np.asarray(v) for k, v in inputs.items()}
    x = np.ascontiguousarray(inp["x"][0])
    pa, pbm, pc, pd = prep_A(x, inp), prep_B(x, inp), prep_C(inp), prep_D(inp)
    ident = np.eye(128, dtype=np.float32)
    in_maps = []
    for c in range(NCORES):
        m = {"ident": ident, "gidx": _gather_idx(c)}
        for pre, mp in (("A_", pa[c]), ("B_", pbm[c]), ("C_", pc[c]), ("D_", pd[c])):
            for k, v in mp.items():
                m[pre + k] = v
        in_maps.append(m)

    nc = bass.Bass("TRN2", target_bir_lowering=False)
    dram = {}
    for name, arr in in_maps[0].items():
        dt = mybir.dt.int32 if arr.dtype == np.int32 else F32
        dram[name] = nc.dram_tensor(name, list(arr.shape), dt, kind="ExternalInput").ap()
    out_ap = nc.dram_tensor("out", [TOK, D], F32, kind="ExternalOutput").ap()
    xa_in = nc.dram_tensor("xa_in", [NCORES * 128, 2 * TOK], BF16)
    xa_out = nc.dram_tensor("xa_out", [NCORES * NCORES * 128, 2 * TOK], BF16)
    xc_in = nc.dram_tensor("xc_in", [NCORES * 128, 2 * TOK], BF16)
    xc_out = nc.dram_tensor("xc_out", [NCORES * NCORES * 128, 2 * TOK], BF16)
    ag_in = nc.dram_tensor("ag_in", [128, 9 * TOK], BF16)
    ag_out = nc.dram_tensor("ag_out", [NCORES * 128, 9 * TOK], BF16)
    h1T = nc.dram_tensor("h1T", [TOK // TT, 128, 16, TT], F32)
    rg = [list(range(NCORES))]

    P = Prog(nc)
    env = make_env(nc, P)
    load_ident(env, dram["ident"])

    def sub_io(pre):
        return {k[len(pre):]: v for k, v in dram.items() if k.startswith(pre)}

    def allgather(src, dst, rkey, wkey):
        P.collective(lambda e: e.collective_compute("AllGather", ALU.bypass, replica_groups=rg,
                                                    ins=[src.ap().opt()], outs=[dst.ap().opt()]),
                     reads=[rkey], writes=[wkey])

    def head_view(t):
        return t.ap().rearrange("(c p) (h t) -> c p h t", p=128, h=2)

    def gather_view(t):
        return t.ap().rearrange("r (h b t) -> (r h b) t", h=2, b=TOK // TT)

    with contextlib.ExitStack() as es:
        io = sub_io("A_")
        io["o_out"] = head_view(xa_in)
        dil_phase(env, io, NameScope("A", nc, es))
        P.barrier()
        P.flush()
    allgather(xa_in, xa_out, "xa_in", "xa_out")
    with contextlib.ExitStack() as es:
        io = sub_io("B_")
        io.update({"ot_gview": gather_view(xa_out), "ot_key": "xa_out", "gidx": dram["gidx"],
                   "ag_in": ag_in.ap().rearrange("p (s t) -> p s t", s=9), "h1T_out": h1T.ap()})
        rowlocal_phase(env, "B", io, NameScope("B", nc, es))
        P.barrier()
        P.flush()
    allgather(ag_in, ag_out, "ag_in_all", "ag_out")
    with contextlib.ExitStack() as es:
        io = sub_io("C_")
        io["ag_out"] = ag_out.ap().rearrange("(r p) (s t) -> r p s t", p=128, s=9)
        io["o_out"] = head_view(xc_in)
        mla_phase(env, io, NameScope("C", nc, es))
        P.barrier()
        P.flush()
    allgather(xc_in, xc_out, "xc_in", "xc_out")
    with contextlib.ExitStack() as es:
        io = sub_io("D_")
        io.update({"ot_gview": gather_view(xc_out), "ot_key": "xc_out", "gidx": dram["gidx"],
                   "res_in": h1T.ap(), "out": out_ap})
        rowlocal_phase(env, "D", io, NameScope("D", nc, es))
        P.flush(final=True)

    res = run_bass_kernel_spmd(nc, in_maps, core_ids=list(range(NCORES)))
    out = np.concatenate([r["out"] for r in res.results], axis=0)
    return out.reshape(1, S, D).astype(np.float32)
```

```python
import contextlib
import numpy as np
import ml_dtypes
import concourse.bass as bass
import concourse.mybir as mybir
from concourse.bass_utils import run_bass_kernel_spmd

F32 = mybir.dt.float32
BF16 = mybir.dt.bfloat16
AF = mybir.ActivationFunctionType
ALU = mybir.AluOpType

NCORES = 8
S = 16384
D = 2048
TOK = S // NCORES
TT = 512
FF = 5632
NF = FF // 128
EPS = 1e-6
ROPE_THETA = 500000.0

SEM_CHUNK = 16000
DMA_SEM_MAX = 30000


class _Op:
    __slots__ = ("eng", "idx", "fn", "deps", "dma", "needs_inc", "ms", "sem", "target", "prev_target", "cc")

    def __init__(self, eng, idx, fn, deps, dma):
        self.eng = eng
        self.idx = idx
        self.fn = fn
        self.deps = deps
        self.dma = dma
        self.needs_inc = False
        self.ms = None
        self.sem = None
        self.target = None
        self.prev_target = None
        self.cc = False


class Prog:
    ENGS = ("pe", "act", "dve", "pool", "sp")

    def __init__(self, nc):
        self.nc = nc
        self.ops = {e: [] for e in self.ENGS}
        self.last_writer = {}
        self.readers = {}

    def add(self, eng, fn, reads=(), writes=(), dma=False):
        deps = set()
        for b in reads:
            w = self.last_writer.get(b)
            if w is not None:
                deps.add(w)
            if isinstance(b, tuple) and b[0] == "pb":
                rs = self.readers.get(b)
                if rs:
                    deps.update(v for k, v in rs.items() if k != eng)
        for b in writes:
            w = self.last_writer.get(b)
            if w is not None:
                deps.add(w)
            rs = self.readers.get(b)
            if rs:
                deps.update(rs.values())
        idx = len(self.ops[eng])
        key = (eng, idx)
        if eng == "pe" and not dma:
            pe_ops = self.ops["pe"]
            deps = {d for d in deps if not (d[0] == "pe" and not pe_ops[d[1]].dma)}
        op = _Op(eng, idx, fn, deps, dma)
        self.ops[eng].append(op)
        for b in writes:
            self.last_writer[b] = key
            self.readers[b] = {}
        for b in reads:
            rd = self.readers.setdefault(b, {})
            if dma:
                rd[key] = key
            else:
                rd[eng] = key
        return op

    def dma(self, out, in_, reads=(), writes=(), q="sp"):
        return self.add(q, lambda e: e.dma_start(out=out, in_=in_), reads, writes, dma=True)

    def collective(self, fn, reads=(), writes=()):
        op = self.add("pool", fn, reads, writes, dma=True)
        op.cc = True
        return op

    def barrier(self):
        deps = set()
        b0 = getattr(self, "_bar_pos", {e: 0 for e in self.ENGS})
        for e in self.ENGS:
            lastc = None
            for op in reversed(self.ops[e][b0[e]:]):
                if op.dma:
                    deps.add((e, op.idx))
                elif lastc is None:
                    lastc = op
                    deps.add((e, op.idx))
        for e in self.ENGS:
            idx = len(self.ops[e])
            op = _Op(e, idx, lambda eng: eng.nop(), set(deps), False)
            self.ops[e].append(op)
        self._bar_pos = {e: len(self.ops[e]) for e in self.ENGS}
        self.last_writer = {}
        self.readers = {}

    def emit(self):
        self.flush(final=True)

    def flush(self, final=False):
        nc = self.nc
        ops = self.ops
        if not hasattr(self, "_pos"):
            self._pos = {e: 0 for e in self.ENGS}
            self._nms = {e: 0 for e in self.ENGS}
            self.comp_sems = {e: [] for e in self.ENGS}
            self._dpool = {}
            self._known = {e: {} for e in self.ENGS}
            self._ndma = {e: 0 for e in self.ENGS}
        seg = {e: ops[e][self._pos[e]:] for e in self.ENGS}
        for e in self.ENGS:
            for op in seg[e]:
                for (de, di) in op.deps:
                    ops[de][di].needs_inc = True
        for e in self.ENGS:
            for op in seg[e]:
                if op.dma:
                    continue
                if op.needs_inc:
                    chunk, off = divmod(self._nms[e], SEM_CHUNK)
                    while len(self.comp_sems[e]) <= chunk:
                        self.comp_sems[e].append(nc.alloc_semaphore(name=f"s_{e}_{len(self.comp_sems[e])}"))
                    op.ms = (chunk, off + 1)
                    self._nms[e] += 1
        POOL = {"sp": 20, "pool": 10, "act": 2, "pe": 1, "dve": 1}
        for e in self.ENGS:
            for k, op in enumerate(seg[e]):
                if op.cc:
                    op.sem = nc.alloc_semaphore(name=f"cc_{e}_{self._pos[e] + k}")
                    op.prev_target = 0
                    op.target = 1
            dmas = [op for op in seg[e] if op.dma and not op.cc]
            if not dmas:
                continue
            if e not in self._dpool:
                self._dpool[e] = [[nc.alloc_semaphore(name=f"d_{e}_{i}"), 0] for i in range(POOL[e])]
            pool = self._dpool[e]
            for op in dmas:
                k = self._ndma[e]
                self._ndma[e] += 1
                slot = pool[k % len(pool)]
                if slot[1] + 16 > DMA_SEM_MAX:
                    slot[0] = nc.alloc_semaphore(name=f"d_{e}_x{k}")
                    slot[1] = 0
                op.sem = slot[0]
                op.prev_target = slot[1]
                slot[1] += 16
                op.target = slot[1]
        engmap = {"pe": "tensor", "act": "scalar", "dve": "vector", "pool": "gpsimd", "sp": "sync"}
        with nc.Block() as block:
            for e in self.ENGS:
                if seg[e] or final:
                    getattr(block, engmap[e])(self._make_emitter(e, seg[e], final))
        for e in self.ENGS:
            self._pos[e] = len(ops[e])

    def _make_emitter(self, e, seg, final):
        ops = self.ops
        comp_sems = self.comp_sems
        known = self._known[e]

        def emitter(eng):
            def wait(sem, val):
                kid = id(sem)
                if known.get(kid, 0) >= val:
                    return
                eng.wait_ge(sem, val)
                known[kid] = val

            for op in seg:
                if op.deps:
                    need = {}
                    for (de, di) in op.deps:
                        p = ops[de][di]
                        if p.dma:
                            sem, val = p.sem, p.target
                        else:
                            sem, val = comp_sems[de][p.ms[0]], p.ms[1]
                        kid = id(sem)
                        if kid not in need or need[kid][1] < val:
                            need[kid] = (sem, val)
                    for sem, val in need.values():
                        wait(sem, val)
                if op.cc:
                    ins = op.fn(eng)
                    ins.then_inc(op.sem)
                elif op.dma:
                    if op.prev_target:
                        wait(op.sem, op.prev_target)
                    ins = op.fn(eng)
                    ins.then_inc(op.sem, 16)
                else:
                    ins = op.fn(eng)
                    if op.needs_inc:
                        ins.then_inc(comp_sems[e][op.ms[0]], 1)
            if final:
                last = {}
                for op in ops[e]:
                    if op.dma:
                        last[id(op.sem)] = (op.sem, op.target)
                for sem, tgt in last.values():
                    wait(sem, tgt)

        return emitter


class Ring:
    def __init__(self, items):
        self.items = list(items)
        self.i = 0

    def next(self):
        it = self.items[self.i % len(self.items)]
        self.i += 1
        return it


class WStream:
    def __init__(self, P, nc, nbuf, cols, name, st):
        self.P = P
        self.nbuf = nbuf
        self.name = st.name(name)
        self.bufs = [st.alloc(f"{name}{i}", [128, cols], BF16) for i in range(nbuf)]
        self.i = 0

    def load(self, src, ncols, parts=128):
        b = self.i % self.nbuf
        self.i += 1
        self.P.dma(self.bufs[b][0:parts, 0:ncols], src, writes=[(self.name, b)], q="pool")
        return b


def run_jobs(ws, jobs, hook=None):
    loaded = {}
    nxt = 0
    outstanding = 0
    n = len(jobs)
    for i in range(n):
        while nxt < n and (outstanding < ws.nbuf or jobs[nxt][0] is None):
            if jobs[nxt][0] is not None:
                loaded[nxt] = jobs[nxt][0](ws)
                outstanding += 1
            nxt += 1
            if nxt > i and outstanding >= ws.nbuf:
                break
        b = loaded.pop(i, None)
        jobs[i][1](b)
        if jobs[i][0] is not None:
            outstanding -= 1
        if hook is not None:
            hook(i)


class Env:
    pass


def make_env(nc, P):
    env = Env()
    env.nc = nc
    env.P = P
    env.pb = [nc.alloc_psum_tensor(f"pb{i}", [128, 512], F32) for i in range(8)]
    env.ones_f = nc.alloc_sbuf_tensor("ones_f", [128, 128], F32)
    env.ones_b = nc.alloc_sbuf_tensor("ones_b", [128, 128], BF16)
    env.ident_f = nc.alloc_sbuf_tensor("ident_f", [128, 128], F32)
    env.ident_b = nc.alloc_sbuf_tensor("ident_b", [128, 128], BF16)
    P.add("dve", lambda e: e.memset(env.ones_f[:], 1.0), writes=["ones_f"])
    P.add("dve", lambda e: e.memset(env.ones_b[:], 1.0), writes=["ones_b"])
    return env


def load_ident(env, ident_ap):
    env.P.dma(env.ident_f[:], ident_ap, writes=["ident_f"])
    env.P.dma(env.ident_b[:], ident_ap, writes=["ident_b"], q="pool")


def load_const(env, name, dram_ap, shape, dtype=F32, q=None, st=None):
    t = st.alloc(name, shape, dtype) if st is not None else env.nc.alloc_sbuf_tensor(name, shape, dtype)
    if st is not None:
        name = st.name(name)
    if q is None:
        q = "pool" if dtype != F32 else "sp"
    env.P.dma(t[:], dram_ap, writes=[name], q=q)
    return t


def mm(P, out, lhsT, rhs, start, stop, reads, writes, **kw):
    P.add("pe", lambda e: e.matmul(out, lhsT=lhsT, rhs=rhs, start=start, stop=stop, **kw), reads=reads, writes=writes)


def rowlocal_phase(env, mode, io, st):
    nc, P, pb = env.nc, env.P, env.pb
    acc = Ring([0, 1, 2, 3])
    tpb = Ring([5, 6])
    SSB = 4
    MISC = 7

    hT = st.alloc("hT", [128, 16, TT], F32)
    xnT = st.alloc("xnT", [128, 16, TT], BF16)
    OT = st.alloc("OT", [128, 16, TT], BF16)
    aT = st.alloc("aT", [128, NF, TT], BF16)
    sqs = [st.alloc(f"sq{i}", [128, TT], F32) for i in range(3)]
    sqr = Ring(range(3))
    rs = st.alloc("rs", [128, TT], F32)
    rs2 = st.alloc("rs2", [128, TT], F32)
    ws = WStream(P, nc, 3, NF * 128, "wr", st)
    g_ffn = load_const(env, "g_ffn", io["g_ffn"], [128, 16], st=st)
    if "ot_gview" in io:
        gidx = load_const(env, "gidx", io["gidx"], [128, 64], mybir.dt.int32, q="sp", st=st)
    ident_f = env.ident_f
    if mode == "B":
        xs = [st.alloc(f"xs{i}", [128, D], F32) for i in range(2)]
        g_attn1 = load_const(env, "g_attn1", io["g_attn1"], [128, 16], st=st)
        g_kv = load_const(env, "g_kv", io["g_kv"], [128, 16], st=st)
        g_qa = load_const(env, "g_qa", io["g_qa"], [128, 4], st=st)
        g_kva = load_const(env, "g_kva", io["g_kva"], [128, 4], st=st)
        rm64 = load_const(env, "rm64", io["rm64"], [64, 64], BF16, st=st)
        craw = st.alloc("craw", [128, 4, TT], F32)
        cst = st.alloc("cst", [128, 4, TT], BF16)
        kraw = st.alloc("kraw", [64, TT], F32)
        kraw_b = st.alloc("kraw_b", [64, TT], BF16)
        kt1 = st.alloc("kt1", [64, TT], F32)
        kt2 = st.alloc("kt2", [64, TT], F32)
        kout = st.alloc("kout", [64, TT], BF16)
        cosT = st.alloc("cosT", [64, TT], F32)
        sinT = st.alloc("sinT", [64, TT], F32)
    else:
        g_fin = load_const(env, "g_fin", io["g_fin"], [128, 16], st=st)
        osb = [st.alloc(f"osb{i}", [128, D], F32) for i in range(2)]
    sgs = [st.alloc(f"sg{i}", [128, TT], F32) for i in range(2)]
    sgr = Ring(range(2))

    def hk(k):
        return ("hT", k)

    def norm_stats(src_fn, nchunk, keys, dim, dst, dstkey):
        for k in range(nchunk):
            r = sqr.next()
            P.add("act", lambda e, k=k, r=r: e.activation(out=sqs[r][:], in_=src_fn(k), func=AF.Square),
                  reads=[keys(k)], writes=[("sq", r)])
            mm(P, pb[SSB][:], env.ones_f[:], sqs[r][:], k == 0, k == nchunk - 1,
               reads=[("sq", r), "ones_f"], writes=[("pb", SSB)])
        P.add("act", lambda e: e.activation(out=dst[:], in_=pb[SSB][:], func=AF.Sqrt, scale=1.0 / dim, bias=EPS),
              reads=[("pb", SSB)], writes=[dstkey])
        P.add("dve", lambda e: e.reciprocal(out=dst[:], in_=dst[:]), reads=[dstkey], writes=[dstkey])

    def apply_norm(g, rsbuf, rskey):
        for k in range(16):
            P.add("dve", lambda e, k=k: e.scalar_tensor_tensor(out=xnT[:, k, :], in0=hT[:, k, :], scalar=g[:, k:k + 1],
                                                               in1=rsbuf[:], op0=ALU.mult, op1=ALU.mult),
                  reads=[hk(k), rskey], writes=[("xnT", k)])

    jobs = []
    for t in range(DBG.get("ntile", TOK // TT)):
        t0 = t * TT

        def init_res(_b, t=t, t0=t0):
            if mode == "B":
                for sub in range(4):
                    xb = sub % 2
                    P.dma(xs[xb][:], io["res_in"][t0 + sub * 128:t0 + (sub + 1) * 128, :], writes=[("xs", xb)])
                    for grp in range(4):
                        bk = tpb.next()
                        for j in range(4):
                            k = grp * 4 + j
                            P.add("pe", lambda e, k=k, j=j, bk=bk, xb=xb: e.transpose(
                                pb[bk][:, j * 128:(j + 1) * 128], xs[xb][:, k * 128:(k + 1) * 128], ident_f[:]),
                                reads=[("xs", xb), "ident_f"], writes=[("pb", bk)])
                        P.add("act", lambda e, grp=grp, bk=bk, sub=sub: e.activation(
                            out=hT[:, grp * 4:grp * 4 + 4, sub * 128:(sub + 1) * 128],
                            in_=pb[bk][:].rearrange("p (a b) -> p a b", b=128), func=AF.Copy),
                            reads=[("pb", bk)], writes=[hk(grp * 4 + j) for j in range(4)])
            else:
                P.dma(hT[:], io["res_in"][t], writes=[hk(k) for k in range(16)])
            if "ot_gview" in io:
                for ih in range(16):
                    col = ih * 4 + t
                    P.add("pool", lambda e, ih=ih, col=col: e.indirect_dma_start(
                        out=OT[:, ih, :], out_offset=None, in_=io["ot_gview"],
                        in_offset=bass.IndirectOffsetOnAxis(ap=gidx[:, col:col + 1], axis=0)),
                        reads=[io["ot_key"], st.name("gidx")], writes=[("OT", ih % 2)], dma=True)
            else:
                for hh in range(2):
                    P.dma(OT[:].rearrange("p (i h) t -> p i h t", h=2)[:, :, hh, :],
                          io["ot_in"][:, :, hh, t0:t0 + TT].rearrange("i p t -> p i t"), writes=[("OT", hh)])
        jobs.append((None, init_res))

        for dt in range(16):
            def ld(w, dt=dt):
                return w.load(io["wo_t"][dt], 16 * 128)

            def cp(b, dt=dt):
                wv = ws.bufs[b][:, 0:16 * 128].rearrange("p (h j) -> p h j", j=128)
                bk = acc.next()
                for h in range(16):
                    mm(P, pb[bk][:], wv[:, h, :], OT[:, h, :], h == 0, h == 15,
                       reads=[(ws.name, b), ("OT", 0), ("OT", 1)], writes=[("pb", bk)])
                P.add("dve", lambda e: e.tensor_tensor(out=hT[:, dt, :], in0=hT[:, dt, :], in1=pb[bk][:], op=ALU.add),
                      reads=[hk(dt), ("pb", bk)], writes=[hk(dt)])
            jobs.append((ld, cp))

        def ffn_norm(_b):
            norm_stats(lambda k: hT[:, k, :], 16, hk, D, rs, "rs")
            apply_norm(g_ffn, rs, "rs")
        jobs.append((None, ffn_norm))
        for ft in range(NF):
            def ld(w, ft=ft):
                return w.load(io["wgu_t"][ft], 2 * 16 * 128)

            def cp(b, ft=ft):
                wv = ws.bufs[b][:, 0:2 * 16 * 128].rearrange("p (g k j) -> p g k j", g=2, j=128)
                bg = acc.next()
                bu = acc.next()
                for k in range(16):
                    mm(P, pb[bg][:], wv[:, 0, k, :], xnT[:, k, :], k == 0, k == 15,
                       reads=[(ws.name, b), ("xnT", k)], writes=[("pb", bg)])
                for k in range(16):
                    mm(P, pb[bu][:], wv[:, 1, k, :], xnT[:, k, :], k == 0, k == 15,
                       reads=[(ws.name, b), ("xnT", k)], writes=[("pb", bu)])
                r = sgr.next()
                P.add("act", lambda e: e.activation(out=sgs[r][:], in_=pb[bg][:], func=AF.Silu),
                      reads=[("pb", bg)], writes=[("sg", r)])
                P.add("dve", lambda e: e.tensor_tensor(out=aT[:, ft, :], in0=sgs[r][:], in1=pb[bu][:], op=ALU.mult),
                      reads=[("sg", r), ("pb", bu)], writes=[("aT", ft)])
            jobs.append((ld, cp))
        for dt in range(16):
            def ld(w, dt=dt):
                return w.load(io["wdn_t"][dt], NF * 128)

            def cp(b, dt=dt):
                wv = ws.bufs[b][:, 0:NF * 128].rearrange("p (f j) -> p f j", j=128)
                bk = acc.next()
                for f in range(NF):
                    mm(P, pb[bk][:], wv[:, f, :], aT[:, f, :], f == 0, f == NF - 1,
                       reads=[(ws.name, b), ("aT", f)], writes=[("pb", bk)])
                P.add("dve", lambda e: e.tensor_tensor(out=hT[:, dt, :], in0=hT[:, dt, :], in1=pb[bk][:], op=ALU.add),
                      reads=[hk(dt), ("pb", bk)], writes=[hk(dt)])
            jobs.append((ld, cp))

        if mode == "B":
            def pre1(_b, t=t):
                P.dma(io["h1T_out"][t], hT[:], reads=[hk(k) for k in range(16)], writes=[("h1T_out", t)])
                norm_stats(lambda k: hT[:, k, :], 16, hk, D, rs, "rs")
                apply_norm(g_attn1, rs, "rs")
            jobs.append((None, pre1))

            def lat_jobs(wkey, gq, seg0, t0=t0):
                for qt in range(4):
                    def ld(w, qt=qt):
                        return w.load(io[wkey][qt], 16 * 128)

                    def cp(b, qt=qt):
                        wv = ws.bufs[b][:, 0:16 * 128].rearrange("p (k j) -> p k j", j=128)
                        bk = acc.next()
                        for k in range(16):
                            mm(P, pb[bk][:], wv[:, k, :], xnT[:, k, :], k == 0, k == 15,
                               reads=[(ws.name, b), ("xnT", k)], writes=[("pb", bk)])
                        P.add("act", lambda e: e.activation(out=craw[:, qt, :], in_=pb[bk][:], func=AF.Copy),
                              reads=[("pb", bk)], writes=[("craw", qt)])
                    jobs.append((ld, cp))

                def fin(_b):
                    norm_stats(lambda k: craw[:, k, :], 4, lambda k: ("craw", k), 512, rs2, "rs2")
                    for qt in range(4):
                        P.add("dve", lambda e, qt=qt: e.scalar_tensor_tensor(
                            out=cst[:, qt, :], in0=craw[:, qt, :], scalar=gq[:, qt:qt + 1], in1=rs2[:],
                            op0=ALU.mult, op1=ALU.mult), reads=[("craw", qt), "rs2"], writes=[("cst", qt)])
                    P.dma(io["ag_in"][:, seg0:seg0 + 4, t0:t0 + TT], cst[:],
                          reads=[("cst", q) for q in range(4)], writes=[("ag_in", seg0, t0)])
                jobs.append((None, fin))
            lat_jobs("wqa_t", g_qa, 0)

            def pre2(_b):
                apply_norm(g_kv, rs, "rs")
            jobs.append((None, pre2))
            lat_jobs("wkva_t", g_kva, 4)

            def ldk(w):
                return w.load(io["wkpe_t"], 16 * 64)

            def cpk(b, t=t, t0=t0):
                wv = ws.bufs[b][:, 0:16 * 64].rearrange("p (k j) -> p k j", j=64)
                bk = acc.next()
                for k in range(16):
                    mm(P, pb[bk][0:64, :], wv[:, k, :], xnT[:, k, :], k == 0, k == 15,
                       reads=[(ws.name, b), ("xnT", k)], writes=[("pb", bk)])
                P.dma(cosT[:], io["cosB"][t], writes=["cosT"])
                P.dma(sinT[:], io["sinB"][t], writes=["sinT"])
                P.add("act", lambda e: e.activation(out=kraw[:], in_=pb[bk][0:64, :], func=AF.Copy),
                      reads=[("pb", bk)], writes=["kraw"])
                P.add("act", lambda e: e.activation(out=kraw_b[:], in_=pb[bk][0:64, :], func=AF.Copy),
                      reads=[("pb", bk)], writes=["kraw_b"])
                mm(P, pb[MISC][0:64, :], rm64[:], kraw_b[:], True, True, reads=["kraw_b", st.name("rm64")],
                   writes=[("pb", MISC)])
                P.add("dve", lambda e: e.tensor_tensor(out=kt1[:], in0=kraw[:], in1=cosT[:], op=ALU.mult),
                      reads=["kraw", "cosT"], writes=["kt1"])
                P.add("dve", lambda e: e.tensor_tensor(out=kt2[:], in0=sinT[:], in1=pb[MISC][0:64, :], op=ALU.mult),
                      reads=["sinT", ("pb", MISC)], writes=["kt2"])
                P.add("dve", lambda e: e.tensor_tensor(out=kout[:], in0=kt1[:], in1=kt2[:], op=ALU.add),
                      reads=["kt1", "kt2"], writes=["kout"])
                P.dma(io["ag_in"][0:64, 8, t0:t0 + TT], kout[:], reads=["kout"], writes=[("ag_in", 8, t0)])
            jobs.append((ldk, cpk))
        else:
            def fin(_b, t0=t0):
                norm_stats(lambda k: hT[:, k, :], 16, hk, D, rs, "rs")
                for k in range(16):
                    P.add("dve", lambda e, k=k: e.scalar_tensor_tensor(
                        out=hT[:, k, :], in0=hT[:, k, :], scalar=g_fin[:, k:k + 1], in1=rs[:],
                        op0=ALU.mult, op1=ALU.mult), reads=[hk(k), "rs"], writes=[hk(k)])
                for sub in range(4):
                    ob = sub % 2
                    for grp in range(4):
                        bk = tpb.next()
                        for j in range(4):
                            k = grp * 4 + j
                            P.add("pe", lambda e, k=k, j=j, bk=bk, sub=sub: e.transpose(
                                pb[bk][:, j * 128:(j + 1) * 128], hT[:, k, sub * 128:(sub + 1) * 128], ident_f[:]),
                                reads=[hk(k), "ident_f"], writes=[("pb", bk)])
                        P.add("act", lambda e, grp=grp, bk=bk, ob=ob: e.activation(
                            out=osb[ob][:, grp * 512:(grp + 1) * 512], in_=pb[bk][:], func=AF.Copy),
                            reads=[("pb", bk)], writes=[("osb", ob)])
                    P.dma(io["out"][t0 + sub * 128:t0 + (sub + 1) * 128, :], osb[ob][:],
                          reads=[("osb", ob)], writes=[("out", t0, sub)])
            jobs.append((None, fin))
    run_jobs(ws, jobs)


class NameScope:
    def __init__(self, prefix, nc=None, stack=None):
        self.prefix = prefix
        self.nc = nc
        self.stack = stack

    def enter_name(self, n):
        return f"{self.prefix}_{n}"

    def name(self, n):
        return f"{self.prefix}_{n}"

    def alloc(self, n, shape, dt):
        if self.stack is None:
            return self.nc.alloc_sbuf_tensor("sb" + self.name(n), shape, dt)
        return self.stack.enter_context(self.nc.sbuf_tensor("sb" + self.name(n), shape, dt))


def mla_phase(env, io, st):
    nc, P, pb = env.nc, env.P, env.pb
    SC = (128 + 64) ** -0.5
    sbank = Ring([0, 1, 2, 3])
    prj = Ring([4, 7])
    OB, DB = 5, 6
    MISC = 7
    KnT = st.alloc("KnT", [128, S], BF16)
    KpT = st.alloc("KpT", [64, S], BF16)
    V = st.alloc("V", [128, S // 128, 128], BF16)
    cqT = [st.alloc(f"cqT{i}", [128, 4, TT], BF16) for i in range(2)]
    cT = [st.alloc(f"cT{i}", [128, 4, TT], BF16) for i in range(2)]
    QnT = [st.alloc(f"QnT{i}", [128, TT], BF16) for i in range(2)]
    QpT = [st.alloc(f"QpT{i}", [64, TT], BF16) for i in range(2)]
    qraw = st.alloc("qraw", [64, TT], F32)
    qraw_b = st.alloc("qraw_b", [64, TT], BF16)
    qt1 = st.alloc("qt1", [64, TT], F32)
    qt2 = st.alloc("qt2", [64, TT], F32)
    cosq = [st.alloc(f"cosq{i}", [64, TT], F32) for i in range(2)]
    sinq = [st.alloc(f"sinq{i}", [64, TT], F32) for i in range(2)]
    PT = [st.alloc(f"PT{i}", [128, TT], BF16) for i in range(6)]
    ptr = Ring(range(6))
    rden = st.alloc("rden", [128, TT], F32)
    dacc = [st.alloc(f"dacc{i}", [128, TT], F32) for i in range(2)]
    Oout = [st.alloc(f"Oout{i}", [128, TT], BF16) for i in range(2)]
    wqb = load_const(env, "wqb", io["wqb_t"], [128, 2, 4 * 192], BF16, st=st)
    wkvb = load_const(env, "wkvb", io["wkvb_t"], [128, 2, 4 * 256], BF16, st=st)
    rm64 = load_const(env, "rm64", io["rm64"], [64, 64], BF16, st=st)
    tri = load_const(env, "tri", io["tri"], [128, 128], BF16, st=st)
    NB = DBG.get("nb", S // TT)

    def proj(hl, b):
        wq = wqb[:, hl, :].rearrange("p (k j) -> p k j", j=192)
        wk = wkvb[:, hl, :].rearrange("p (k j) -> p k j", j=256)
        r, off = divmod(b * TT, TOK)
        cb = b % 2
        P.dma(cqT[cb][:], io["ag_out"][r, :, 0:4, off:off + TT], reads=["ag_out"], writes=[("cqT", cb)])
        P.dma(cT[cb][:], io["ag_out"][r, :, 4:8, off:off + TT], reads=["ag_out"], writes=[("cT", cb)])
        P.dma(cosq[cb][:], io["cosB_all"][:, b * TT:(b + 1) * TT], writes=[("cosq", cb)])
        P.dma(sinq[cb][:], io["sinB_all"][:, b * TT:(b + 1) * TT], writes=[("sinq", cb)])
        if hl == 0:
            P.dma(KpT[:, b * TT:(b + 1) * TT], io["ag_out"][r, 0:64, 8, off:off + TT], reads=["ag_out"], writes=[("KpT", b)])
        bk = prj.next()
        for k in range(4):
            mm(P, pb[bk][:], wq[:, k, 0:128], cqT[cb][:, k, :], k == 0, k == 3,
               reads=[("cqT", cb), st.name("wqb")], writes=[("pb", bk)])
        act(P, QnT[cb][:], pb[bk][:], AF.Copy, reads=[("pb", bk)], writes=[("QnT", cb)])
        bk = prj.next()
        for k in range(4):
            mm(P, pb[bk][0:64, :], wq[:, k, 128:192], cqT[cb][:, k, :], k == 0, k == 3,
               reads=[("cqT", cb), st.name("wqb")], writes=[("pb", bk)])
        act(P, qraw[:], pb[bk][0:64, :], AF.Copy, reads=[("pb", bk)], writes=["qraw"])
        act(P, qraw_b[:], pb[bk][0:64, :], AF.Copy, reads=[("pb", bk)], writes=["qraw_b"])
        bk2 = prj.next()
        for k in range(4):
            mm(P, pb[bk2][:], wk[:, k, 0:128], cT[cb][:, k, :], k == 0, k == 3,
               reads=[("cT", cb), st.name("wkvb")], writes=[("pb", bk2)])
        act(P, KnT[:, b * TT:(b + 1) * TT], pb[bk2][:], AF.Copy, reads=[("pb", bk2)], writes=[("KnT", b)])
        bk3 = prj.next()
        for sub in range(4):
            for k in range(4):
                mm(P, pb[bk3][:, sub * 128:(sub + 1) * 128], cT[cb][:, k, sub * 128:(sub + 1) * 128], wk[:, k, 128:256],
                   sub == 0 and k == 0, k == 3, reads=[("cT", cb), st.name("wkvb")], writes=[("pb", bk3)],
                   skip_group_check=True)
        act(P, V[:, 4 * b:4 * b + 4, :], pb[bk3][:].rearrange("p (a j) -> p a j", j=128), AF.Copy,
            reads=[("pb", bk3)], writes=[("V", b)])
        rb = prj.next()
        mm(P, pb[rb][0:64, :], rm64[:], qraw_b[:], True, True, reads=["qraw_b", st.name("rm64")], writes=[("pb", rb)])
        tt(P, qt1[:], qraw[:], cosq[cb][:], ALU.mult, reads=["qraw", ("cosq", cb)], writes=["qt1"])
        tt(P, qt2[:], sinq[cb][:], pb[rb][0:64, :], ALU.mult, reads=[("sinq", cb), ("pb", rb)], writes=["qt2"])
        tt(P, QpT[cb][:], qt1[:], qt2[:], ALU.add, reads=["qt1", "qt2"], writes=[("QpT", cb)])

    def attn(hl, b, mid_hook):
        r, off = divmod(b * TT, TOK)
        cb = b % 2
        nch = 4 * b + 4
        SK = 3
        pend = []

        def s_step(c):
            j = c - 4 * b
            lo = 128 * j if j > 0 else 0
            sb = sbank.next()
            mm(P, pb[sb][:, lo:TT], KnT[:, c * 128:(c + 1) * 128], QnT[cb][:, lo:TT], True, False,
               reads=[("KnT", c // 4), ("QnT", cb)], writes=[("pb", sb)])
            mm(P, pb[sb][:, lo:TT], KpT[:, c * 128:(c + 1) * 128], QpT[cb][:, lo:TT], False, True,
               reads=[("KpT", c // 4), ("QpT", cb)], writes=[("pb", sb)])
            pr = ptr.next()
            act(P, PT[pr][:, lo:TT], pb[sb][:, lo:TT], AF.Exp, reads=[("pb", sb)], writes=[("PT", pr)], scale=SC)
            if j >= 0:
                tt(P, PT[pr][:, lo:lo + 128], PT[pr][:, lo:lo + 128], tri[:], ALU.mult,
                   reads=[("PT", pr), st.name("tri")], writes=[("PT", pr)])
            return (c, pr, lo)

        def pv_step(item):
            c, pr, lo = item
            mm(P, pb[OB][:, lo:TT], V[:, c, :], PT[pr][:, lo:TT], c == 0, c == nch - 1,
               reads=[("V", c // 4), ("PT", pr)], writes=[("pb", OB)], skip_group_check=True)
            a = c % 2
            eng = "dve" if a == 0 else "pool"
            if c < 2:
                if lo > 0:
                    P.add(eng, lambda e: e.memset(dacc[a][:, 0:lo], 0.0), writes=[("dacc", a)])
                P.add(eng, lambda e: e.tensor_copy(out=dacc[a][:, lo:TT], in_=PT[pr][:, lo:TT]),
                      reads=[("PT", pr)], writes=[("dacc", a)])
            else:
                tt(P, dacc[a][:, lo:TT], dacc[a][:, lo:TT], PT[pr][:, lo:TT], ALU.add, reads=[("PT", pr), ("dacc", a)],
                   writes=[("dacc", a)], eng=eng)

        for c in range(nch):
            pend.append(s_step(c))
            if len(pend) > SK:
                pv_step(pend.pop(0))
            if c == min(5, nch - 1):
                mid_hook()
        while pend:
            pv_step(pend.pop(0))
        ob = b % 2
        mm(P, pb[DB][:], env.ones_f[:], dacc[0][:], True, False, reads=["ones_f", ("dacc", 0)], writes=[("pb", DB)])
        mm(P, pb[DB][:], env.ones_f[:], dacc[1][:], False, True, reads=["ones_f", ("dacc", 1)], writes=[("pb", DB)])
        P.add("dve", lambda e: e.reciprocal(out=rden[:], in_=pb[DB][:]), reads=[("pb", DB)], writes=["rden"])
        tt(P, Oout[ob][:], rden[:], pb[OB][:], ALU.mult, reads=["rden", ("pb", OB)], writes=[("Oout", ob)])
        P.dma(io["o_out"][r, :, hl, off:off + TT], Oout[ob][:], reads=[("Oout", ob)], writes=[("o_out", b, hl)])

    for hl in range(DBG.get("nhl", 2)):
        proj(hl, 0)
        for b in range(NB):
            nxt = (lambda hl=hl, b=b: proj(hl, b + 1)) if b + 1 < NB else (lambda: None)
            attn(hl, b, nxt)


A_GROUPS = ((128, 1), (512, 4), (2048, 16))
DBG = {}
CH = 2048


def act(P, out, in_, func, reads, writes, **kw):
    P.add("act", lambda e: e.activation(out=out, in_=in_, func=func, **kw), reads=reads, writes=writes)


def tt(P, out, in0, in1, op, reads, writes, eng="dve"):
    P.add(eng, lambda e: e.tensor_tensor(out=out, in0=in0, in1=in1, op=op), reads=reads, writes=writes)


def tr(P, out, in_, ident, reads, writes):
    P.add("pe", lambda e: e.transpose(out, in_, ident), reads=reads, writes=writes)


def dil_phase(env, io, st):
    nc, P, pb = env.nc, env.P, env.pb
    SC = 128 ** -0.5
    acc = Ring([0, 1, 2])
    tbr = Ring([6, 7])
    SB = Ring([3, 4, 5])
    xs = [st.alloc(f"xs{i}", [128, D], F32) for i in range(2)]
    xb = [st.alloc(f"xb{i}", [128, D], BF16) for i in range(2)]
    ssq = [st.alloc(f"ssq{i}", [128, 4], F32) for i in range(4)]
    ssr = Ring(range(4))
    xnT = [st.alloc(f"xnT{i}", [128, 16, TT], BF16) for i in range(2)]
    wA = st.alloc("wA", [128, 9, 16 * 128], BF16)
    gA = load_const(env, "gA", io["g_attn0"], [128, 16], st=st)
    rm32 = load_const(env, "rm32", io["rm32"], [32, 32], BF16, st=st)
    masks = load_const(env, "masks", io["masks"], [128, 3, 512], BF16, st=st)
    QT = [st.alloc(f"QT{g}", [128, CH], BF16) for g in range(3)]
    KT = [st.alloc(f"KT{g}", [128, 2, CH], BF16) for g in range(3)]
    VT = [st.alloc(f"VT{g}", [128, CH], BF16) for g in range(3)]
    VK = [st.alloc(f"VK{g}", [128, 2, 16, 128], BF16) for g in range(3)]
    raw = [st.alloc(f"raw{i}", [32, TT], BF16) for i in range(2)]
    rt1 = [st.alloc(f"rt1{i}", [32, TT], F32) for i in range(2)]
    rawr = Ring(range(2))
    rt2 = st.alloc("rt2", [32, TT], F32)
    cosA = [st.alloc(f"cosA{i}", [32, TT], F32) for i in range(2)]
    sinA = [st.alloc(f"sinA{i}", [32, TT], F32) for i in range(2)]
    PT = [st.alloc(f"PT{i}", [128, 512], BF16) for i in range(3)]
    ptr = Ring(range(3))
    accO = st.alloc("accO", [128, CH], F32)
    accD = st.alloc("accD", [128, CH], F32)
    Oout = st.alloc("Oout", [128, CH], BF16)
    for g in range(3):
        P.add("pool", lambda e, g=g: e.memset(KT[g][:], 0.0), writes=[("KT", g, 0), ("KT", g, 1)])
        P.add("pool", lambda e, g=g: e.memset(VK[g][:], 0.0), writes=[("VK", g, 0), ("VK", g, 1)])
    NCH = S // CH
    ident_b = env.ident_b
    gkey = st.name("gA")

    def prep(pos, sub):
        sx = sub % 2
        sq = ssq[ssr.next()]
        sk = ("ssq", id(sq))
        P.dma(xs[sx][:], io["x"][pos + sub * 128:pos + (sub + 1) * 128, :], writes=[("xs", sx)])
        P.add("dve", lambda e: e.memset(sq[:, 0:1], 0.0), writes=[sk])
        act(P, xb[sx][:], xs[sx][:], AF.Square, reads=[("xs", sx), sk], writes=[("xb", sx), sk], accum_out=sq[:, 0:1])
        act(P, sq[:, 1:2], sq[:, 0:1], AF.Sqrt, reads=[sk], writes=[sk], scale=1.0 / D, bias=EPS)
        P.add("dve", lambda e: e.reciprocal(out=sq[:, 2:3], in_=sq[:, 1:2]), reads=[sk], writes=[sk])
        P.add("dve", lambda e: e.tensor_scalar(out=xb[sx][:], in0=xs[sx][:], scalar1=sq[:, 2:3],
                                               scalar2=None, op0=ALU.mult),
              reads=[("xs", sx), sk], writes=[("xb", sx)])

    def norm_steps(p0, xi, last):
        steps = []
        for sub in range(4):
            for half in range(2):
                def step(sub=sub, half=half):
                    sx = sub % 2
                    if sub == 0 and half == 0:
                        P.dma(cosA[xi][:], io["cosA"][:, p0:p0 + TT], writes=[("cosA", xi)])
                        P.dma(sinA[xi][:], io["sinA"][:, p0:p0 + TT], writes=[("sinA", xi)])
                    bk = tbr.next()
                    pv = pb[bk][:].bitcast(BF16)
                    for j in range(8):
                        k = half * 8 + j
                        tr(P, pv[:, j * 128:(j + 1) * 128], xb[sx][:, k * 128:(k + 1) * 128], ident_b[:],
                           reads=[("xb", sx), "ident_b"], writes=[("pb", bk)])
                    k0 = half * 8
                    tt(P, xnT[xi][:, k0:k0 + 8, sub * 128:(sub + 1) * 128], pv.rearrange("p (a b) -> p a b", b=128),
                       gA[:, k0:k0 + 8].unsqueeze(2).to_broadcast([128, 8, 128]), ALU.mult,
                       reads=[("pb", bk), gkey], writes=[("xnT", xi, k0 + j) for j in range(8)])
                    if half == 1:
                        if sub < 2:
                            prep(p0, sub + 2)
                        elif not last:
                            prep(p0 + TT, sub - 2)
                steps.append(step)
        return steps

    def qkv_tile(hl, par, tt_, xi, nsteps):
        jobs = []
        c0 = tt_ * TT
        pend = []

        def flush_rope():
            while pend:
                pend.pop(0)()

        for g in range(3):
            for which in range(3):
                def cp(b, g=g, which=which):
                    wi = g * 3 + which
                    wv = wA[:, wi, :].rearrange("p (k j) -> p k j", j=128)
                    bk = acc.next()
                    for k in range(16):
                        mm(P, pb[bk][:], wv[:, k, :], xnT[xi][:, k, :], k == 0, k == 15,
                           reads=[("wA", wi), ("xnT", xi, k)], writes=[("pb", bk)])
                    flush_rope()
                    if which == 2:
                        act(P, VT[g][:, c0:c0 + TT], pb[bk][:], AF.Copy, reads=[("pb", bk)], writes=[("VT", g, tt_)])
                        return
                    if which == 0:
                        dst, dst32, dkey = QT[g][:, c0:c0 + TT], QT[g][0:32, c0:c0 + TT], ("QT", g, tt_)
                    else:
                        dst, dst32, dkey = KT[g][:, par, c0:c0 + TT], KT[g][0:32, par, c0:c0 + TT], ("KT", g, par)
                    rw = rawr.next()
                    act(P, raw[rw][:], pb[bk][0:32, :], AF.Copy, reads=[("pb", bk)], writes=[("raw", rw)])
                    act(P, rt1[rw][:], pb[bk][0:32, :], AF.Copy, reads=[("pb", bk)], writes=[("rt1", rw)])
                    act(P, dst, pb[bk][:], AF.Copy, reads=[("pb", bk)], writes=[dkey])

                    def fin(rw=rw, dst32=dst32, dkey=dkey):
                        rb = SB.next()
                        mm(P, pb[rb][0:32, :], rm32[:], raw[rw][:], True, True, reads=[("raw", rw), st.name("rm32")],
                           writes=[("pb", rb)])
                        tt(P, rt1[rw][:], rt1[rw][:], cosA[xi][:], ALU.mult, reads=[("cosA", xi), ("rt1", rw)], writes=[("rt1", rw)])
                        tt(P, rt2[:], sinA[xi][:], pb[rb][0:32, :], ALU.mult, reads=[("sinA", xi), ("pb", rb)], writes=["rt2"])
                        tt(P, dst32, rt1[rw][:], rt2[:], ALU.add, reads=[("rt1", rw), "rt2", dkey], writes=[dkey])
                    pend.append(fin)
                jobs.append((None, cp))

        def hook(i):
            if i < len(nsteps):
                nsteps[i]()
        for i, (_ld, cp) in enumerate(jobs):
            cp(None)
            hook(i)
        flush_rope()
        for i in range(len(jobs), len(nsteps)):
            nsteps[i]()

    def prev_blocks(g, dil, par, n):
        if dil == 1:
            if n > 0:
                return KT[g][:, par, (n - 1) * 128:n * 128], VK[g][:, par, n - 1, :]
            return KT[g][:, 1 - par, CH - 128:CH], VK[g][:, 1 - par, 15, :]
        if dil == 4:
            jj, rr = divmod(n, 4)
            if jj > 0:
                return _blk(KT[g], par, (jj - 1) * 4 + rr, dil), VK[g][:, par, (jj - 1) * 4 + rr, :]
            return _blk(KT[g], 1 - par, 12 + rr, dil), VK[g][:, 1 - par, 12 + rr, :]
        return _blk(KT[g], 1 - par, n, dil), VK[g][:, 1 - par, n, :]

    def attn_chunk(hl, C, par):
        for g, (_win, dil) in enumerate(A_GROUPS):
            qkeys = [("QT", g, tq) for tq in range(4)]
            kkeys = [("KT", g, 0), ("KT", g, 1)]
            vkeys = [("VK", g, 0), ("VK", g, 1)]
            for q4 in range(4):
                bk = tbr.next()
                pv = pb[bk][:].bitcast(BF16)
                for j in range(4):
                    n = q4 * 4 + j
                    tr(P, pv[:, j * 128:(j + 1) * 128], _blk(VT[g], None, n, dil), ident_b[:],
                       reads=[("VT", g, tq) for tq in range(4)] + ["ident_b"], writes=[("pb", bk)])
                act(P, VK[g][:, par, q4 * 4:q4 * 4 + 4, :], pv[:, 0:512].rearrange("p (a j) -> p a j", j=128), AF.Copy,
                    reads=[("pb", bk)], writes=[("VK", g, par)])
            def s_part(q2, g=g, dil=dil):
                sb = SB.next()
                vps = []
                for u in range(2):
                    n = q2 * 2 + u
                    qsrc = _blk(QT[g], None, n, dil)
                    kprev, vprev = prev_blocks(g, dil, par, n)
                    vps.append(vprev)
                    kcur = _blk(KT[g], par, n, dil)
                    mm(P, pb[sb][:, u * 256:u * 256 + 128], kprev, qsrc, True, True,
                       reads=kkeys + qkeys, writes=[("pb", sb)], skip_group_check=True)
                    mm(P, pb[sb][:, u * 256 + 128:u * 256 + 256], kcur, qsrc, True, True,
                       reads=kkeys + qkeys, writes=[("pb", sb)], skip_group_check=True)
                pr = ptr.next()
                act(P, PT[pr][:], pb[sb][:], AF.Exp, reads=[("pb", sb)], writes=[("PT", pr)], scale=SC)
                fm = _first_mask_sel(g, q2, C)
                tt(P, PT[pr][:], PT[pr][:], masks[:, fm, :], ALU.mult, reads=[("PT", pr), st.name("masks")],
                   writes=[("PT", pr)])
                return (q2, pr, vps)

            def pv_part(item, g=g, dil=dil):
                q2, pr, vps = item
                ob = acc.next()
                for u in range(2):
                    n = q2 * 2 + u
                    vcur = VK[g][:, par, n, :]
                    o_ap = pb[ob][:, u * 128:(u + 1) * 128]
                    d_ap = pb[ob][:, 256 + u * 128:256 + (u + 1) * 128]
                    rk = vkeys + [("PT", pr)]
                    mm(P, o_ap, vps[u], PT[pr][:, u * 256:u * 256 + 128], u == 0, False, reads=rk, writes=[("pb", ob)],
                       skip_group_check=True)
                    mm(P, o_ap, vcur, PT[pr][:, u * 256 + 128:u * 256 + 256], False, True, reads=rk, writes=[("pb", ob)],
                       skip_group_check=True)
                    mm(P, d_ap, env.ones_b[:], PT[pr][:, u * 256:u * 256 + 128], False, False,
                       reads=["ones_b", ("PT", pr)], writes=[("pb", ob)], skip_group_check=True)
                    mm(P, d_ap, env.ones_b[:], PT[pr][:, u * 256 + 128:u * 256 + 256], False, True,
                       reads=["ones_b", ("PT", pr)], writes=[("pb", ob)], skip_group_check=True)
                dO = _acc_dst(accO, q2, dil)
                dD = _acc_dst(accD, q2, dil)
                src_o = pb[ob][:, 0:256].rearrange("p (u i) -> p u i", i=128)
                src_d = pb[ob][:, 256:512].rearrange("p (u i) -> p u i", i=128)
                if g == 0:
                    act(P, dO, src_o, AF.Copy, reads=[("pb", ob)], writes=["accO"])
                    act(P, dD, src_d, AF.Copy, reads=[("pb", ob)], writes=["accD"])
                else:
                    tt(P, dO, dO, src_o, ALU.add, reads=[("pb", ob), "accO"], writes=["accO"])
                    tt(P, dD, dD, src_d, ALU.add, reads=[("pb", ob), "accD"], writes=["accD"])

            pend = []
            for q2 in range(8):
                pend.append(s_part(q2))
                if len(pend) > 2:
                    pv_part(pend.pop(0))
            while pend:
                pv_part(pend.pop(0))
        P.add("dve", lambda e: e.reciprocal(out=accD[:], in_=accD[:]), reads=["accD"], writes=["accD"])
        tt(P, Oout[:], accO[:], accD[:], ALU.mult, reads=["accO", "accD"], writes=["Oout"])
        P.dma(io["o_out"][C, :, hl, :], Oout[:], reads=["Oout"], writes=[("o_out", C, hl)])

    NT = CH // TT
    for hl in range(DBG.get("nhl", 2)):
        for wi in range(9):
            P.dma(wA[:, wi, :], io["wa_t"][hl, wi // 3, wi % 3], writes=[("wA", wi)], q="pool")
        nchk = DBG.get("nch", NCH)
        ntiles = nchk * NT
        prep(0, 0)
        prep(0, 1)
        for stp in norm_steps(0, 0, ntiles == 1):
            stp()
        for t in range(ntiles):
            C, tt_ = divmod(t, NT)
            par = C % 2
            nxt = norm_steps((t + 1) * TT, (t + 1) % 2, t + 2 >= ntiles) if t + 1 < ntiles else []
            qkv_tile(hl, par, tt_, t % 2, nxt)
            if tt_ == NT - 1:
                attn_chunk(hl, C, par)


def _blk(t, par, n, dil):
    if dil == 1:
        lo, step = n * 128, 1
    elif dil == 4:
        jj, rr = divmod(n, 4)
        lo, step = jj * 512 + rr, 4
    else:
        lo, step = n, 16
    hi = lo + step * 127 + 1
    if par is None:
        return t[:, lo:hi:step]
    return t[:, par, lo:hi:step]


def _acc_dst(t, q2, dil):
    if dil == 1:
        return t[:, q2 * 256:(q2 + 1) * 256].rearrange("p (u i) -> p u i", i=128)
    if dil == 4:
        jj, r0 = divmod(q2 * 2, 4)
        return t[:, jj * 512:(jj + 1) * 512].rearrange("p (i r) -> p r i", r=4)[:, r0:r0 + 2, :]
    return t[:, :].rearrange("p (i r) -> p r i", r=16)[:, q2 * 2:q2 * 2 + 2, :]


def _first_mask_sel(g, q2, C):
    if C != 0:
        return 1
    if g == 0:
        return 2 if q2 == 0 else 1
    if g == 1:
        return 0 if q2 < 2 else 1
    return 0


def _tile_w(w, ncoltile=128):
    K, N = w.shape
    kc = K // 128
    nt = N // ncoltile
    return np.ascontiguousarray(w.reshape(kc, 128, nt, ncoltile).transpose(2, 1, 0, 3).reshape(nt, 128, kc * ncoltile))


def _gT(g):
    return np.ascontiguousarray(g.reshape(-1, 128).T)


def _rope_tables(dim, positions):
    inv_freq = ROPE_THETA ** (-np.arange(0, dim, 2, dtype=np.float32) / np.float32(dim))
    inv_freq = inv_freq.astype(np.float32)
    ang = positions.astype(np.float32)[:, None] * inv_freq[None, :]
    cos = np.cos(ang).astype(np.float32)
    sin = np.sin(ang).astype(np.float32)
    cosT = np.concatenate([cos, cos], axis=1).T
    sinT = np.concatenate([sin, sin], axis=1).T
    return np.ascontiguousarray(cosT), np.ascontiguousarray(sinT)


def _rot_mat(dim):
    half = dim // 2
    R = np.zeros((dim, dim), np.float32)
    for e in range(half):
        R[e + half, e] = -1.0
        R[e, e + half] = 1.0
    return R


def _masks():
    kk = np.arange(128)[:, None]
    qq = np.arange(128)[None, :]
    prev = (kk >= qq).astype(np.float32)
    cur = (kk <= qq).astype(np.float32)
    zero = np.zeros_like(prev)
    reg = np.concatenate([prev, cur, prev, cur], axis=1)
    first = np.concatenate([zero, cur, zero, cur], axis=1)
    half = np.concatenate([zero, cur, prev, cur], axis=1)
    return np.ascontiguousarray(np.stack([first, reg, half], axis=1))


NPDT = {F32: np.float32, BF16: ml_dtypes.bfloat16}


def launch(build_fn, in_maps, out_specs):
    nc = bass.Bass("TRN2", target_bir_lowering=False)
    io = {}
    for name, arr in in_maps[0].items():
        dt = BF16 if arr.dtype == ml_dtypes.bfloat16 else F32
        io[name] = nc.dram_tensor(name, list(arr.shape), dt, kind="ExternalInput").ap()
    for name, (shape, dt) in out_specs.items():
        io[name] = nc.dram_tensor(name, list(shape), dt, kind="ExternalOutput").ap()
    P = Prog(nc)
    env = make_env(nc, P)
    load_ident(env, io["ident"])
    build_fn(env, io)
    P.emit()
    if DBG.get("trace"):
        res = run_bass_kernel_spmd(nc, in_maps, core_ids=list(range(NCORES)), trace=True)
        print("exec_time_ns", res.exec_time_ns)
        DBG["res"] = res
    else:
        res = run_bass_kernel_spmd(nc, in_maps, core_ids=list(range(NCORES)))
    return res.results


def prep_rowlocal_weights(w_o, w_gu, w_down):
    gt = _tile_w(w_gu[:, :FF])
    ut = _tile_w(w_gu[:, FF:])
    wgu_t = np.ascontiguousarray(np.stack([gt, ut], axis=2).reshape(NF, 128, 2 * 16 * 128))
    return {"wo_t": _tile_w(w_o), "wgu_t": wgu_t, "wdn_t": _tile_w(w_down)}


def prep_D(inp):
    wts = prep_rowlocal_weights(inp["b_w_o"][0], inp["ffn_w_gu"][1], inp["ffn_w_down"][1])
    common = dict(wts)
    common["g_ffn"] = _gT(inp["ffn_norm_g"][1])
    common["g_fin"] = _gT(inp["final_norm_g"])
    return [dict(common) for _ in range(NCORES)]


def prep_B(xfull, inp):
    wts = prep_rowlocal_weights(inp["a_w_o"][0], inp["ffn_w_gu"][0], inp["ffn_w_down"][0])
    common = dict(wts)
    common["wqa_t"] = _tile_w(inp["b_w_q_a"][0])
    common["wkva_t"] = _tile_w(inp["b_w_kv_a"][:, :512])
    common["wkpe_t"] = _tile_w(inp["b_w_kv_a"][:, 512:576], 64)[0]
    common["g_ffn"] = _gT(inp["ffn_norm_g"][0])
    common["g_attn1"] = _gT(inp["attn_norm_g"][1])
    common["g_kv"] = _gT(inp["kv_norm_g"])
    common["g_qa"] = _gT(inp["b_q_a_norm_g"][0])
    common["g_kva"] = _gT(inp["b_kv_a_norm_g"])
    common["rm64"] = _rot_mat(64)
    maps = []
    for c in range(NCORES):
        m = dict(common)
        m["res_in"] = np.ascontiguousarray(xfull[c * TOK:(c + 1) * TOK])
        cosT, sinT = _rope_tables(64, np.arange(c * TOK, (c + 1) * TOK))
        m["cosB"] = np.ascontiguousarray(cosT.reshape(64, 4, TT).transpose(1, 0, 2))
        m["sinB"] = np.ascontiguousarray(sinT.reshape(64, 4, TT).transpose(1, 0, 2))
        maps.append(m)
    return maps


def _tri():
    kk = np.arange(128)[:, None]
    qq = np.arange(128)[None, :]
    return (kk <= qq).astype(np.float32)


def prep_C(inp):
    cosT, sinT = _rope_tables(64, np.arange(S))
    common = {"rm64": _rot_mat(64), "tri": _tri(), "cosB_all": cosT, "sinB_all": sinT}
    wqb = inp["b_w_q_b"][0]
    wkvb = inp["b_w_kv_b"]
    maps = []
    for c in range(NCORES):
        m = dict(common)
        q = [wqb[:, h * 192:(h + 1) * 192].reshape(4, 128, 192).transpose(1, 0, 2).reshape(128, 768) for h in (2 * c, 2 * c + 1)]
        k = [wkvb[:, h * 256:(h + 1) * 256].reshape(4, 128, 256).transpose(1, 0, 2).reshape(128, 1024) for h in (2 * c, 2 * c + 1)]
        m["wqb_t"] = np.ascontiguousarray(np.stack(q, axis=1))
        m["wkvb_t"] = np.ascontiguousarray(np.stack(k, axis=1))
        maps.append(m)
    return maps


def prep_A(xfull, inp):
    wq = inp["a_w_qkv"][0]
    cosT, sinT = _rope_tables(32, np.arange(S))
    common = {"x": np.ascontiguousarray(xfull), "g_attn0": _gT(inp["attn_norm_g"][0]), "rm32": _rot_mat(32),
              "masks": _masks(), "cosA": cosT, "sinA": sinT}
    maps = []
    for c in range(NCORES):
        m = dict(common)
        wa = np.empty((2, 3, 3, 128, 16 * 128), np.float32)
        for hl in range(2):
            h = 2 * c + hl
            for g in range(3):
                for which in range(3):
                    c0 = g * 6144 + which * 2048 + h * 128
                    wa[hl, g, which] = _tile_w(wq[:, c0:c0 + 128])[0]
        m["wa_t"] = wa
        maps.append(m)
    return maps


def _gather_idx(c):
    idx = np.zeros((128, 64), np.int32)
    p = np.arange(128)
    for i in range(NCORES):
        for h in range(2):
            for tb in range(4):
                idx[:, (i * 2 + h) * 4 + tb] = (((i * NCORES + c) * 128 + p) * 2 + h) * 4 + tb
    return idx


def kernel(**inputs):
    inp = {k: np.asarray(v) for k, v in inputs.items()}
    x = np.ascontiguousarray(inp["x"][0])
    pa, pbm, pc, pd = prep_A(x, inp), prep_B(x, inp), prep_C(inp), prep_D(inp)
    ident = np.eye(128, dtype=np.float32)
    in_maps = []
    for c in range(NCORES):
        m = {"ident": ident, "gidx": _gather_idx(c)}
        for pre, mp in (("A_", pa[c]), ("B_", pbm[c]), ("C_", pc[c]), ("D_", pd[c])):
            for k, v in mp.items():
                m[pre + k] = v
        in_maps.append(m)

    nc = bass.Bass("TRN2", target_bir_lowering=False)
    dram = {}
    for name, arr in in_maps[0].items():
        dt = mybir.dt.int32 if arr.dtype == np.int32 else F32
        dram[name] = nc.dram_tensor(name, list(arr.shape), dt, kind="ExternalInput").ap()
    out_ap = nc.dram_tensor("out", [TOK, D], F32, kind="ExternalOutput").ap()
    xa_in = nc.dram_tensor("xa_in", [NCORES * 128, 2 * TOK], BF16)
    xa_out = nc.dram_tensor("xa_out", [NCORES * NCORES * 128, 2 * TOK], BF16)
    xc_in = nc.dram_tensor("xc_in", [NCORES * 128, 2 * TOK], BF16)
    xc_out = nc.dram_tensor("xc_out", [NCORES * NCORES * 128, 2 * TOK], BF16)
    ag_in = nc.dram_tensor("ag_in", [128, 9 * TOK], BF16)
    ag_out = nc.dram_tensor("ag_out", [NCORES * 128, 9 * TOK], BF16)
    h1T = nc.dram_tensor("h1T", [TOK // TT, 128, 16, TT], F32)
    rg = [list(range(NCORES))]

    P = Prog(nc)
    env = make_env(nc, P)
    load_ident(env, dram["ident"])

    def sub_io(pre):
        return {k[len(pre):]: v for k, v in dram.items() if k.startswith(pre)}

    def allgather(src, dst, rkey, wkey):
        P.collective(lambda e: e.collective_compute("AllGather", ALU.bypass, replica_groups=rg,
                                                    ins=[src.ap().opt()], outs=[dst.ap().opt()]),
                     reads=[rkey], writes=[wkey])

    def head_view(t):
        return t.ap().rearrange("(c p) (h t) -> c p h t", p=128, h=2)

    def gather_view(t):
        return t.ap().rearrange("r (h b t) -> (r h b) t", h=2, b=TOK // TT)

    with contextlib.ExitStack() as es:
        io = sub_io("A_")
        io["o_out"] = head_view(xa_in)
        dil_phase(env, io, NameScope("A", nc, es))
        P.barrier()
        P.flush()
    allgather(xa_in, xa_out, "xa_in", "xa_out")
    with contextlib.ExitStack() as es:
        io = sub_io("B_")
        io.update({"ot_gview": gather_view(xa_out), "ot_key": "xa_out", "gidx": dram["gidx"],
                   "ag_in": ag_in.ap().rearrange("p (s t) -> p s t", s=9), "h1T_out": h1T.ap()})
        rowlocal_phase(env, "B", io, NameScope("B", nc, es))
        P.barrier()
        P.flush()
    allgather(ag_in, ag_out, "ag_in_all", "ag_out")
    with contextlib.ExitStack() as es:
        io = sub_io("C_")
        io["ag_out"] = ag_out.ap().rearrange("(r p) (s t) -> r p s t", p=128, s=9)
        io["o_out"] = head_view(xc_in)
        mla_phase(env, io, NameScope("C", nc, es))
        P.barrier()
        P.flush()
    allgather(xc_in, xc_out, "xc_in", "xc_out")
    with contextlib.ExitStack() as es:
        io = sub_io("D_")
        io.update({"ot_gview": gather_view(xc_out), "ot_key": "xc_out", "gidx": dram["gidx"],
                   "res_in": h1T.ap(), "out": out_ap})
        rowlocal_phase(env, "D", io, NameScope("D", nc, es))
        P.flush(final=True)

    res = run_bass_kernel_spmd(nc, in_maps, core_ids=list(range(NCORES)))
    out = np.concatenate([r["out"] for r in res.results], axis=0)
    return out.reshape(1, S, D).astype(np.float32)
```
